# Optimizing a Trainium2 kernel written in Bass

```python
import jax, jax.numpy as jnp
from jax import lax
import numpy as np

D_MODEL = 1024
BATCH = 4
SEQ = 8192
DEPTH = 1

NSA_HEADS = 8
NSA_KV_GROUPS = 2
HEAD_DIM = 64
NSA_WIDTH = NSA_HEADS * HEAD_DIM
KV_WIDTH = NSA_KV_GROUPS * HEAD_DIM
CMP_LEN = 32
CMP_STRIDE = 16
CMP_HIDDEN = 256
SLC_LEN = 64
SLC_TOPK = 16
WINDOW = 512
Q_BLOCK = 128
ROPE_THETA = 10000.0
RWKV_HEADS = 8
RWKV_HEAD = 64
RWKV_WIDTH = RWKV_HEADS * RWKV_HEAD
DECAY_LORA = 32
AAA_LORA = 32
GATE_LORA = 96
LNX_EPS = 64e-5
D_FF = ((-(-8 * D_MODEL // 3) + 255) // 256) * 256
NORM_EPS = 1e-6
BIG = 1e30

SPLIT_SIZES = (NSA_WIDTH, KV_WIDTH, KV_WIDTH, KV_WIDTH, KV_WIDTH, KV_WIDTH, KV_WIDTH, 3 * NSA_HEADS,
               RWKV_WIDTH, RWKV_WIDTH, RWKV_WIDTH, DECAY_LORA, AAA_LORA, GATE_LORA, 2 * D_MODEL)
IN_COLS = sum(SPLIT_SIZES)

kernel_name = 'hybrid_nsa_rwkv7_block'


def _split_points():
    pts, acc = [], 0
    for s in SPLIT_SIZES[:-1]:
        acc += s
        pts.append(acc)
    return pts


def rms_norm(x, g):
    xf = x.astype(jnp.float32)
    y = xf * lax.rsqrt(jnp.mean(xf * xf, axis=-1, keepdims=True) + NORM_EPS)
    return (y * g.astype(jnp.float32)).astype(x.dtype)


def rope_tables(n, dim):
    inv = 1.0 / (ROPE_THETA ** (jnp.arange(0, dim, 2, dtype=jnp.float32) / dim))
    ang = jnp.arange(n, dtype=jnp.float32)[:, None] * inv[None, :]
    return jnp.cos(ang), jnp.sin(ang)


def apply_rope(x, cos, sin):
    x1, x2 = jnp.split(x.astype(jnp.float32), 2, axis=-1)
    c, s = cos[None, :, None, :], sin[None, :, None, :]
    return jnp.concatenate([x1 * c - x2 * s, x1 * s + x2 * c], axis=-1).astype(x.dtype)


def token_shift(z, mu):
    prev = jnp.pad(z, ((0, 0), (1, 0), (0, 0)))[:, :-1]
    return z + (prev - z) * mu


def masked_softmax(s, mask):
    s = jnp.where(mask, s.astype(jnp.float32), -BIG)
    return jnp.where(mask, jax.nn.softmax(s, axis=-1), 0.0)


def compress(z, pe, w1, b1, w2):
    bsz, seq, groups, dim = z.shape
    n_cmp = (seq - CMP_LEN) // CMP_STRIDE + 1
    idx = jnp.arange(n_cmp)[:, None] * CMP_STRIDE + jnp.arange(CMP_LEN)[None, :]
    blk = z[:, idx] + pe[None, None, :, None, :]
    blk = blk.transpose(0, 1, 3, 2, 4).reshape(bsz, n_cmp, groups, CMP_LEN * dim)
    return jax.nn.gelu(blk @ w1 + b1) @ w2


def nsa_attention(q, k_cmp, v_cmp, k_slc, v_slc, k_win, v_win, gate_logits,
                  pe_k, w1_k, b1_k, w2_k, pe_v, w1_v, b1_v, w2_v):
    bsz, seq = q.shape[:2]
    G, R, d = NSA_KV_GROUPS, NSA_HEADS // NSA_KV_GROUPS, HEAD_DIM
    scale = d ** -0.5
    cos, sin = rope_tables(seq, d)

    kc = compress(k_cmp, pe_k, w1_k, b1_k, w2_k)
    vc = compress(v_cmp, pe_v, w1_v, b1_v, w2_v)
    n_cmp = kc.shape[1]
    cmp_start = jnp.arange(n_cmp) * CMP_STRIDE
    cmp_end = cmp_start + CMP_LEN - 1
    n_slc = seq // SLC_LEN
    n_sel = min(SLC_TOPK, n_slc)
    slc_start = jnp.arange(n_slc) * SLC_LEN
    overlap = ((cmp_start[:, None] < slc_start[None, :] + SLC_LEN)
               & (cmp_end[:, None] >= slc_start[None, :])).astype(jnp.float32)

    q_plain = q.reshape(bsz, seq, G, R, d)
    q_rot = apply_rope(q, cos, sin).reshape(bsz, seq, G, R, d)
    k_slc_blk = apply_rope(k_slc, cos, sin).reshape(bsz, n_slc, SLC_LEN, G, d).transpose(0, 3, 1, 2, 4)
    v_slc_blk = v_slc.reshape(bsz, n_slc, SLC_LEN, G, d).transpose(0, 3, 1, 2, 4)
    pad = ((0, 0), (WINDOW, 0), (0, 0), (0, 0))
    k_win_pad = jnp.pad(apply_rope(k_win, cos, sin), pad)
    v_win_pad = jnp.pad(v_win, pad)
    gate = jax.nn.sigmoid(gate_logits).reshape(bsz, seq, 3, G, R)

    b_ix = jnp.arange(bsz)[:, None, None, None]
    g_ix = jnp.arange(G)[None, :, None, None]
    blk_ids = jnp.arange(n_slc)
    in_blk = jnp.arange(SLC_LEN)
    win_off = jnp.arange(WINDOW + Q_BLOCK) - WINDOW

    def query_block(i):
        q0 = i * Q_BLOCK
        t = q0 + jnp.arange(Q_BLOCK)
        qp = lax.dynamic_slice_in_dim(q_plain, q0, Q_BLOCK, 1)
        qr = lax.dynamic_slice_in_dim(q_rot, q0, Q_BLOCK, 1)
        gb = lax.dynamic_slice_in_dim(gate, q0, Q_BLOCK, 1)

        s_c = jnp.einsum('bqgrd,bngd->bgrqn', qp, kc) * scale
        p_c = masked_softmax(s_c, cmp_end[None, :] <= t[:, None])
        o_c = jnp.einsum('bgrqn,bngd->bqgrd', p_c.astype(vc.dtype), vc)

        imp = jnp.einsum('bgrqn,nj->bgqj', p_c, overlap)
        cur = (t // SLC_LEN)[:, None]
        imp = jnp.where(blk_ids[None, :] > cur, -BIG, imp)
        imp = jnp.where((blk_ids[None, :] == 0) | (blk_ids[None, :] == cur), BIG, imp)
        _, sel = lax.top_k(imp, n_sel)

        k_sel = k_slc_blk[b_ix, g_ix, sel].reshape(bsz, G, Q_BLOCK, n_sel * SLC_LEN, d)
        v_sel = v_slc_blk[b_ix, g_ix, sel].reshape(bsz, G, Q_BLOCK, n_sel * SLC_LEN, d)
        pos = (sel[..., None] * SLC_LEN + in_blk).reshape(bsz, G, Q_BLOCK, n_sel * SLC_LEN)
        s_s = jnp.einsum('bqgrd,bgqkd->bgrqk', qr, k_sel) * scale
        p_s = masked_softmax(s_s, (pos <= t[:, None])[:, :, None])
        o_s = jnp.einsum('bgrqk,bgqkd->bqgrd', p_s.astype(v_sel.dtype), v_sel)

        k_w = lax.dynamic_slice_in_dim(k_win_pad, q0, WINDOW + Q_BLOCK, 1)
        v_w = lax.dynamic_slice_in_dim(v_win_pad, q0, WINDOW + Q_BLOCK, 1)
        kpos = q0 + win_off
        m_w = ((kpos[None, :] <= t[:, None]) & (kpos[None, :] > t[:, None] - WINDOW)
               & (kpos[None, :] >= 0))
        s_w = jnp.einsum('bqgrd,bkgd->bgrqk', qr, k_w) * scale
        p_w = masked_softmax(s_w, m_w)
        o_w = jnp.einsum('bgrqk,bkgd->bqgrd', p_w.astype(v_w.dtype), v_w)

        o = (gb[:, :, 0, :, :, None] * o_c + gb[:, :, 1, :, :, None] * o_s
             + gb[:, :, 2, :, :, None] * o_w)
        return o.reshape(bsz, Q_BLOCK, NSA_WIDTH)

    out = lax.map(query_block, jnp.arange(seq // Q_BLOCK))
    return out.transpose(1, 0, 2, 3).reshape(bsz, seq, NSA_WIDTH)


def rwkv7_time_mix(r, k, v, w_lo, a_lo, g_lo, mu_r, mu_k, mu_v, mu_w, mu_a, mu_g,
                   w0, w_w2, a0, w_a2, w_g2, k_k, k_a, r_k, lnx_w, lnx_b):
    out_dtype = r.dtype
    bsz, seq, _ = r.shape
    H, N = RWKV_HEADS, RWKV_HEAD
    f32 = jnp.float32
    r = token_shift(r, mu_r).astype(f32)
    k = token_shift(k, mu_k).astype(f32)
    v = token_shift(v, mu_v).astype(f32)
    w_lo = token_shift(w_lo, mu_w)
    a_lo = token_shift(a_lo, mu_a)
    g_lo = token_shift(g_lo, mu_g)

    w_log = -jax.nn.softplus(-(w0 + jnp.tanh(w_lo) @ w_w2).astype(f32)) - 0.5
    decay = jnp.exp(-jnp.exp(w_log))
    a = jax.nn.sigmoid((a0 + a_lo @ w_a2).astype(f32))
    g = jax.nn.sigmoid(g_lo) @ w_g2

    kk = (k * k_k).reshape(bsz, seq, H, N)
    kk = kk / jnp.maximum(jnp.sqrt(jnp.sum(kk * kk, axis=-1, keepdims=True)), 1e-12)
    k = k * (1.0 + (a - 1.0) * k_a)

    def heads(z):
        return z.reshape(bsz, seq, H, N).transpose(1, 0, 2, 3)

    xs = (heads(r), heads(decay), heads(k), heads(v), kk.transpose(1, 0, 2, 3), heads(a))

    def step(state, inp):
        r_t, w_t, k_t, v_t, kk_t, a_t = inp
        s_kk = jnp.einsum('bhij,bhj->bhi', state, kk_t)
        state = (state * w_t[:, :, None, :] - s_kk[..., None] * (kk_t * a_t)[:, :, None, :]
                 + v_t[..., None] * k_t[:, :, None, :])
        return state, jnp.einsum('bhij,bhj->bhi', state, r_t)

    _, y = lax.scan(step, jnp.zeros((bsz, H, N, N), f32), xs)
    y = y.transpose(1, 0, 2, 3)
    mean = jnp.mean(y, axis=-1, keepdims=True)
    var = jnp.mean(jnp.square(y - mean), axis=-1, keepdims=True)
    y = ((y - mean) * lax.rsqrt(var + LNX_EPS)).reshape(bsz, seq, H * N) * lnx_w + lnx_b
    rh, kh, vh = r.reshape(bsz, seq, H, N), k.reshape(bsz, seq, H, N), v.reshape(bsz, seq, H, N)
    bonus = (jnp.sum(rh * kh * r_k, axis=-1, keepdims=True) * vh).reshape(bsz, seq, H * N)
    return ((y + bonus) * g).astype(out_dtype)


def hybrid_layer(x, norm1_pre, norm1_post, w_in,
                 cmp_pe_k, cmp_w1_k, cmp_b1_k, cmp_w2_k, cmp_pe_v, cmp_w1_v, cmp_b1_v, cmp_w2_v,
                 mu_r, mu_k, mu_v, mu_w, mu_a, mu_g, w0, w_w2, a0, w_a2, w_g2,
                 k_k, k_a, r_k, lnx_w, lnx_b, w_branch_a, w_branch_b, w_out,
                 norm2_pre, norm2_post, w_gate, w_up, w_down):
    bsz, seq, _ = x.shape
    h = rms_norm(x, norm1_pre)
    z = h @ w_in
    (q, kc, vc, ks, vs, kw, vw, nsa_g, r, k, v, w_lo, a_lo, g_lo, merge_g) = jnp.split(
        z, _split_points(), axis=-1)

    def kvh(t):
        return t.reshape(bsz, seq, NSA_KV_GROUPS, HEAD_DIM)

    o_a = nsa_attention(q.reshape(bsz, seq, NSA_HEADS, HEAD_DIM), kvh(kc), kvh(vc), kvh(ks), kvh(vs),
                        kvh(kw), kvh(vw), nsa_g,
                        cmp_pe_k, cmp_w1_k, cmp_b1_k, cmp_w2_k, cmp_pe_v, cmp_w1_v, cmp_b1_v, cmp_w2_v)
    o_b = rwkv7_time_mix(r, k, v, w_lo, a_lo, g_lo, mu_r, mu_k, mu_v, mu_w, mu_a, mu_g,
                         w0, w_w2, a0, w_a2, w_g2, k_k, k_a, r_k, lnx_w, lnx_b)

    gate_a, gate_b = jnp.split(jax.nn.sigmoid(merge_g), 2, axis=-1)
    mixed = (gate_a * (o_a @ w_branch_a) + gate_b * (o_b @ w_branch_b)) @ w_out
    x = x + rms_norm(mixed, norm1_post)

    h2 = rms_norm(x, norm2_pre)
    f = (jax.nn.silu(h2 @ w_gate) * (h2 @ w_up)) @ w_down
    return x + rms_norm(f, norm2_post)


def setup_inputs(seed: int = 0) -> dict:
    key = jax.random.key(seed)
    ks = iter(jax.random.split(key, 40))
    f32 = jnp.float32

    def nrm(shape, scale):
        return scale * jax.random.normal(next(ks), shape, f32)

    def unif(shape, lo, hi):
        return jax.random.uniform(next(ks), shape, f32, lo, hi)

    L = DEPTH
    return {
        'x': jax.random.normal(next(ks), (BATCH, SEQ, D_MODEL), f32),
        'norm1_pre': 1.0 + nrm((L, D_MODEL), 0.05),
        'norm1_post': 1.0 + nrm((L, D_MODEL), 0.05),
        'w_in': nrm((L, D_MODEL, IN_COLS), D_MODEL ** -0.5),
        'cmp_pe_k': nrm((L, CMP_LEN, HEAD_DIM), 0.1),
        'cmp_w1_k': nrm((L, CMP_LEN * HEAD_DIM, CMP_HIDDEN), (CMP_LEN * HEAD_DIM) ** -0.5),
        'cmp_b1_k': nrm((L, CMP_HIDDEN), 0.01),
        'cmp_w2_k': nrm((L, CMP_HIDDEN, HEAD_DIM), 2.0 * CMP_HIDDEN ** -0.5),
        'cmp_pe_v': nrm((L, CMP_LEN, HEAD_DIM), 0.1),
        'cmp_w1_v': nrm((L, CMP_LEN * HEAD_DIM, CMP_HIDDEN), (CMP_LEN * HEAD_DIM) ** -0.5),
        'cmp_b1_v': nrm((L, CMP_HIDDEN), 0.01),
        'cmp_w2_v': nrm((L, CMP_HIDDEN, HEAD_DIM), 2.0 * CMP_HIDDEN ** -0.5),
        'mu_r': unif((L, RWKV_WIDTH), 0.0, 1.0),
        'mu_k': unif((L, RWKV_WIDTH), 0.0, 1.0),
        'mu_v': unif((L, RWKV_WIDTH), 0.0, 1.0),
        'mu_w': unif((L, DECAY_LORA), 0.0, 1.0),
        'mu_a': unif((L, AAA_LORA), 0.0, 1.0),
        'mu_g': unif((L, GATE_LORA), 0.0, 1.0),
        'w0': unif((L, RWKV_WIDTH), -3.0, 0.5),
        'w_w2': nrm((L, DECAY_LORA, RWKV_WIDTH), 0.1),
        'a0': nrm((L, RWKV_WIDTH), 0.1),
        'w_a2': nrm((L, AAA_LORA, RWKV_WIDTH), 0.1),
        'w_g2': nrm((L, GATE_LORA, RWKV_WIDTH), GATE_LORA ** -0.5),
        'k_k': 0.85 + nrm((L, RWKV_WIDTH), 0.02),
        'k_a': 1.0 + nrm((L, RWKV_WIDTH), 0.02),
        'r_k': nrm((L, RWKV_HEADS, RWKV_HEAD), 0.1),
        'lnx_w': 1.0 + nrm((L, RWKV_WIDTH), 0.05),
        'lnx_b': nrm((L, RWKV_WIDTH), 0.01),
        'w_branch_a': nrm((L, NSA_WIDTH, D_MODEL), NSA_WIDTH ** -0.5),
        'w_branch_b': nrm((L, RWKV_WIDTH, D_MODEL), RWKV_WIDTH ** -0.5),
        'w_out': nrm((L, D_MODEL, D_MODEL), D_MODEL ** -0.5),
        'norm2_pre': 1.0 + nrm((L, D_MODEL), 0.05),
        'norm2_post': 1.0 + nrm((L, D_MODEL), 0.05),
        'w_gate': nrm((L, D_MODEL, D_FF), D_MODEL ** -0.5),
        'w_up': nrm((L, D_MODEL, D_FF), D_MODEL ** -0.5),
        'w_down': nrm((L, D_FF, D_MODEL), D_FF ** -0.5),
    }


def reference(x, norm1_pre, norm1_post, w_in,
              cmp_pe_k, cmp_w1_k, cmp_b1_k, cmp_w2_k, cmp_pe_v, cmp_w1_v, cmp_b1_v, cmp_w2_v,
              mu_r, mu_k, mu_v, mu_w, mu_a, mu_g, w0, w_w2, a0, w_a2, w_g2,
              k_k, k_a, r_k, lnx_w, lnx_b, w_branch_a, w_branch_b, w_out,
              norm2_pre, norm2_post, w_gate, w_up, w_down):
    layer_params = (norm1_pre, norm1_post, w_in,
                    cmp_pe_k, cmp_w1_k, cmp_b1_k, cmp_w2_k, cmp_pe_v, cmp_w1_v, cmp_b1_v, cmp_w2_v,
                    mu_r, mu_k, mu_v, mu_w, mu_a, mu_g, w0, w_w2, a0, w_a2, w_g2,
                    k_k, k_a, r_k, lnx_w, lnx_b, w_branch_a, w_branch_b, w_out,
                    norm2_pre, norm2_post, w_gate, w_up, w_down)
    for layer in range(DEPTH):
        x = hybrid_layer(x, *[p[layer] for p in layer_params])
    return x
```

```python
import numpy as np
from contextlib import ExitStack
import concourse.bass as bass
import concourse.mybir as mybir
from concourse.bass_utils import run_bass_kernel_spmd

F32 = mybir.dt.float32
BF16 = mybir.dt.bfloat16
I32 = mybir.dt.int32
AF = mybir.ActivationFunctionType
ALU = mybir.AluOpType
AX = mybir.AxisListType


class Sched:
    NDSEM = 12

    def __init__(self, nc, es):
        self.nc = nc
        self.es = es
        self.names = ['pe', 'dve', 'act', 'pool', 'sp']
        self.stream = {e: [] for e in self.names}
        self.cnt = {e: 0 for e in self.names}
        self.sem = {e: es.enter_context(nc.semaphore("s_" + e)) for e in ['pe', 'dve', 'act', 'pool']}
        self.dsem = {q: [es.enter_context(nc.semaphore(f"d_{q}{i}")) for i in range(self.NDSEM)] for q in ['sp', 'pool']}
        self.dval = {q: [0] * self.NDSEM for q in ['sp', 'pool']}
        self.dnext = {q: 0 for q in ['sp', 'pool']}
        self.seen = {e: {} for e in self.names}
        self.state = {}
        self.semobj = {}
        for e, s in self.sem.items():
            self.semobj[('c', e)] = s
        for q in self.dsem:
            for i, s in enumerate(self.dsem[q]):
                self.semobj[('d', q, i)] = s
        self.n_instr = 0
        self.pending = {e: [] for e in self.names}

    def barrier(self):
        evs = [(('c', e), self.cnt[e]) for e in ['pe', 'dve', 'act', 'pool'] if self.cnt[e] > 0]
        for q in self.dsem:
            for i in range(self.NDSEM):
                if self.dval[q][i] > 0:
                    evs.append((('d', q, i), self.dval[q][i]))
        for eng in self.names:
            for ev in evs:
                if ev[0] == ('c', eng):
                    continue
                self._need(eng, ev, self.pending[eng])
        self.state = {}

    def _need(self, eng, ev, waits):
        if ev is None:
            return
        k, v = ev
        if self.seen[eng].get(k, 0) >= v:
            return
        self.seen[eng][k] = v
        waits.append((k, v))

    def _deps(self, eng, reads, writes, is_dma):
        waits = []
        for b in reads:
            st = self.state.get(b)
            if st:
                self._need(eng, st['w'], waits)
        for b in writes:
            st = self.state.get(b)
            if st:
                w = st['w']
                if not (w is not None and eng == 'pe' and w[0] == ('c', 'pe')):
                    self._need(eng, w, waits)
                for k, v in st['r'].items():
                    if k == ('c', 'pe') and eng == 'pe':
                        continue
                    self._need(eng, (k, v), waits)
        return waits

    def _commit(self, ev, reads, writes):
        for b in reads:
            st = self.state.setdefault(b, {'w': None, 'r': {}})
            st['r'][ev[0]] = max(st['r'].get(ev[0], 0), ev[1])
        for b in writes:
            self.state[b] = {'w': ev, 'r': {}}

    def op(self, eng, fn, reads=(), writes=()):
        if eng != 'pe':
            writes = list(writes) + [b for b in reads if b.startswith('ps')]
        waits = self.pending[eng] + self._deps(eng, reads, writes, False)
        self.pending[eng] = []
        self.cnt[eng] += 1
        ev = (('c', eng), self.cnt[eng])
        self.stream[eng].append((waits, fn, ev))
        self._commit(ev, reads, writes)
        self.n_instr += 1

    def dma(self, q, fn, reads=(), writes=()):
        waits = self.pending[q] + self._deps(q, reads, writes, True)
        self.pending[q] = []
        i = self.dnext[q]
        self.dnext[q] = (i + 1) % self.NDSEM
        key = ('d', q, i)
        if self.dval[q][i] > 0:
            self._need(q, (key, self.dval[q][i]), waits)
        self.dval[q][i] += 16
        ev = (key, self.dval[q][i])
        self.stream[q].append((waits, fn, ev))
        self._commit(ev, reads, writes)
        self.n_instr += 1

    def emit(self):
        nc = self.nc
        final_waits = []
        for q in self.dsem:
            for i in range(self.NDSEM):
                if self.dval[q][i] > 0:
                    final_waits.append((('d', q, i), self.dval[q][i]))
        block = self.es.enter_context(nc.Block())
        semobj = self.semobj

        def run(engobj, items, extra=()):
            for waits, fn, ev in items:
                for k, v in waits:
                    engobj.wait_ge(semobj[k], v)
                ins = fn(engobj)
                ins.then_inc(semobj[ev[0]], ev[1] if False else (16 if ev[0][0] == 'd' else 1))
            for k, v in extra:
                engobj.wait_ge(semobj[k], v)

        @block.tensor
        def _(e):
            run(e, self.stream['pe'])

        @block.vector
        def _(e):
            run(e, self.stream['dve'])

        @block.scalar
        def _(e):
            run(e, self.stream['act'])

        @block.gpsimd
        def _(e):
            run(e, self.stream['pool'])

        @block.sync
        def _(e):
            run(e, self.stream['sp'], final_waits)


D = 1024
DFF = 2816
SPLITS = (512, 128, 128, 128, 128, 128, 128, 24, 512, 512, 512, 32, 32, 96, 2048)
OFFS = np.concatenate([[0], np.cumsum(SPLITS)]).astype(int)
(O_Q, O_KC, O_VC, O_KS, O_VS, O_KW, O_VW, O_NG, O_R, O_K, O_V, O_WLO, O_ALO, O_GLO, O_MG) = [int(v) for v in OFFS[:-1]]
NF = 28
TM_COLS = 280 + 2048
NCOLS = NF * 128 + TM_COLS
SCALE = 64 ** -0.5
BIG = 1e30


def _swap_cols(base, nheads):
    idx = []
    for h in range(nheads):
        idx += list(range(base + h * 64 + 32, base + h * 64 + 64)) + list(range(base + h * 64, base + h * 64 + 32))
    return idx


def wcat_columns():
    cols = []
    q = list(range(O_Q, O_Q + 512))
    qs = _swap_cols(O_Q, 8)
    for t in range(4):
        cols += q[t * 128:(t + 1) * 128]
        cols += qs[t * 128:(t + 1) * 128]
    cols += list(range(O_KS, O_KS + 128)) + _swap_cols(O_KS, 2)
    cols += list(range(O_KW, O_KW + 128)) + _swap_cols(O_KW, 2)
    cols += list(range(O_KC, O_KC + 128)) + list(range(O_VC, O_VC + 128))
    cols += list(range(O_R, O_R + 512)) + list(range(O_K, O_K + 512)) + list(range(O_V, O_V + 512))
    cols += list(range(O_WLO, O_WLO + 32)) + [-1] * 32 + list(range(O_ALO, O_ALO + 32)) + [-1] * 32
    cols += list(range(O_GLO, O_GLO + 96)) + [-1] * 32
    assert len(cols) == NF * 128
    cols += list(range(O_VS, O_VS + 128)) + list(range(O_VW, O_VW + 128)) + list(range(O_NG, O_NG + 24))
    cols += list(range(O_MG, O_MG + 2048))
    assert len(cols) == NCOLS
    return np.array(cols)


class Arena:
    def __init__(self, t, n32):
        self.t = t
        self.n32 = n32
        self.off = 0

    def reset(self):
        self.off = 0

    def alloc(self, free_shape, dt=F32):
        n = int(np.prod(free_shape))
        n32 = n if dt == F32 or dt == I32 else (n + 1) // 2
        n32 = (n32 + 15) // 16 * 16
        assert self.off + n32 <= self.n32, (self.off, n32, self.n32)
        v = self.t[:, self.off:self.off + n32]
        self.off += n32
        if dt != F32:
            v = v.bitcast(dt)[:, 0:n]
        else:
            v = v[:, 0:n]
        if len(free_shape) == 2:
            v = v.rearrange("p (a b) -> p a b", a=free_shape[0])
        elif len(free_shape) == 3:
            v = v.rearrange("p (a b c) -> p a b c", a=free_shape[0], b=free_shape[1])
        return v


ARENA_N = 50 * 1024
DBG_STOP = 0


class Ctx:
    pass


def build(S_, dbg=False, phases=(1, 2, 3, 4, 5)):
    nc = bass.Bass("TRN2", target_bir_lowering=False)
    es = ExitStack()
    c = Ctx()
    c.nc, c.es, c.S_, c.dbg = nc, es, S_, dbg
    c.NT, c.NB = S_ // 128, S_ // 512
    c.din = lambda name, shape, dt=F32: nc.dram_tensor(name, shape, dt, kind="ExternalInput").ap()
    c.dsc = lambda name, shape, dt=F32: nc.dram_tensor(name, shape, dt, kind=("ExternalOutput" if dbg else "Internal")).ap()
    c.x_d = c.din("x", [S_, D])
    c.y_d = nc.dram_tensor("y", [S_, D], F32, kind="ExternalOutput").ap()
    arena_t = es.enter_context(nc.sbuf_tensor("arena", [128, ARENA_N], F32))
    c.A = Arena(arena_t, ARENA_N)
    c.ps = [es.enter_context(nc.psum_tensor(f"ps{i}", [128, 512], F32)) for i in range(8)]
    c.pbn = 0
    c.S = Sched(nc, es)

    def nextbank():
        i = c.pbn
        c.pbn = (c.pbn + 1) % 8
        return i
    c.nextbank = nextbank
    if 1 in phases:
        phase1(c)
    if 2 in phases:
        phase2(c)
    if 3 in phases:
        phase3(c)
    if 4 in phases:
        phase4(c)
    if 5 in phases:
        phase5(c)
    c.S.emit()
    return nc, c


def make_ident(c, ident, identf):
    S = c.S
    S.op('pool', lambda e: e.memset(identf, 1.0), writes=['identf'])
    S.op('pool', lambda e: e.affine_select(out=identf, in_=identf, pattern=[[-1, 128]], compare_op=ALU.is_equal, fill=0.0, base=0, channel_multiplier=1), reads=['identf'], writes=['identf'])
    S.op('dve', lambda e: e.tensor_copy(out=ident, in_=identf), reads=['identf'], writes=['ident'])


def rms_rstd(c, src, junk, ss, dim, rk, wk, eps=1e-6):
    S = c.S
    S.op('act', lambda e: e.activation(out=junk, in_=src, func=AF.Square, accum_out=ss), reads=rk, writes=[wk, 'junkx'])
    S.op('dve', lambda e: e.tensor_scalar(out=ss, in0=ss, scalar1=1.0 / dim, scalar2=eps, op0=ALU.mult, op1=ALU.add), reads=[wk], writes=[wk])
    S.op('act', lambda e: e.activation(out=ss, in_=ss, func=AF.Sqrt), reads=[wk], writes=[wk])
    S.op('dve', lambda e: e.reciprocal(out=ss, in_=ss), reads=[wk], writes=[wk])


def load_weight_bf16(c, Wb, w_d, nchunk, ncols, gcol, wst, tag):
    S = c.S
    k = 0
    for ch in range(nchunk):
        for p0 in range(0, ncols, 2048):
            p1 = min(ncols, p0 + 2048)
            st = wst[k % 2]
            sk = f'wst{k % 2}'
            k += 1
            S.dma('sp', lambda e, st=st, ch=ch, p0=p0, p1=p1: e.dma_start(out=st[:, 0:p1 - p0], in_=w_d[:, ch, p0:p1]), writes=[sk])
            if gcol is not None:
                S.op('dve', lambda e, st=st, ch=ch, p0=p0, p1=p1: e.tensor_scalar(out=Wb[:, ch, p0:p1], in0=st[:, 0:p1 - p0], scalar1=gcol[:, ch:ch + 1], scalar2=None, op0=ALU.mult), reads=[sk, 'gcol' + tag], writes=[tag])
            else:
                S.op('dve', lambda e, st=st, ch=ch, p0=p0, p1=p1: e.tensor_copy(out=Wb[:, ch, p0:p1], in_=st[:, 0:p1 - p0]), reads=[sk], writes=[tag])


def phase1(c):
    nc, S, A, S_ = c.nc, c.S, c.A, c.S_
    wcat_d = c.din("wcat", [128, 8, NCOLS])
    g1_d = c.din("g1", [128, 8])
    cos_d = c.din("cosT", [128, S_])
    sin_d = c.din("sinT", [128, S_])
    c.QT_d = c.dsc("QT", [512, S_], BF16)
    c.QrT_d = c.dsc("QrT", [512, S_], BF16)
    c.KsrT_d = c.dsc("KsrT", [128, S_], BF16)
    c.KwrT_d = c.dsc("KwrT", [128, S_], BF16)
    c.kcT_d = c.dsc("kcT", [128, S_], BF16)
    c.vcT_d = c.dsc("vcT", [128, S_], BF16)
    c.zTf_d = c.dsc("zTf", [14 * 128, S_], F32)
    c.V_d = c.dsc("Vtm", [S_, 256], BF16)
    c.G_d = c.dsc("Gtm", [S_, 24], F32)
    c.MG_d = c.dsc("MG", [S_, 2048], F32)
    A.reset()
    cs = A.alloc([512])
    sn = A.alloc([512])
    t1 = A.alloc([512])
    t2 = A.alloc([512])
    Wb = A.alloc([8, NCOLS], BF16)
    g1 = A.alloc([8])
    ident = A.alloc([128], BF16)
    identf = A.alloc([128])
    wst = [A.alloc([2048]) for _ in range(2)]
    xt = [A.alloc([1024]) for _ in range(2)]
    xs = [A.alloc([1024], BF16) for _ in range(2)]
    junk = A.alloc([1024], BF16)
    ss = [A.alloc([1]) for _ in range(2)]
    hT = A.alloc([8, 512], BF16)
    NST = 4
    stf = [A.alloc([512]) for _ in range(NST)]
    stb = [A.alloc([512], BF16) for _ in range(NST)]
    sg = [A.alloc([24]) for _ in range(2)]
    make_ident(c, ident, identf)
    S.dma('sp', lambda e: e.dma_start(out=g1, in_=g1_d), writes=['gcolWb'])
    load_weight_bf16(c, Wb, wcat_d, 8, NCOLS, g1, wst, 'Wb')
    if DBG_STOP == 1:
        return
    kst = [0]

    def nst():
        i = kst[0]
        kst[0] = (i + 1) % NST
        return i
    cpk = [0]

    def evac_copy(dst, src, reads, writes):
        cpk[0] += 1
        if cpk[0] % 2:
            S.op('act', lambda e: e.copy(out=dst, in_=src), reads=reads, writes=writes)
        else:
            S.op('dve', lambda e: e.tensor_copy(out=dst, in_=src), reads=reads, writes=writes)

    for tb in range(c.NB):
        tsl = slice(tb * 512, (tb + 1) * 512)
        S.dma('sp', lambda e, tsl=tsl: e.dma_start(out=cs, in_=cos_d[:, tsl]), writes=['cs'])
        S.dma('sp', lambda e, tsl=tsl: e.dma_start(out=sn, in_=sin_d[:, tsl]), writes=['sn'])
        for i in range(4):
            tt = tb * 4 + i
            p = tt % 2
            S.dma('sp', lambda e, p=p, tt=tt: e.dma_start(out=xt[p], in_=c.x_d[tt * 128:(tt + 1) * 128, :]), writes=[f'xt{p}'])
            rms_rstd(c, xt[p], junk, ss[p], D, [f'xt{p}'], f'ss{p}')
            S.op('act', lambda e, p=p: e.activation(out=xs[p], in_=xt[p], func=AF.Copy, scale=ss[p]), reads=[f'xt{p}', f'ss{p}'], writes=[f'xs{p}'])
            b = c.nextbank()
            pT = c.ps[b][:].bitcast(BF16).rearrange("p (a b) -> p a b", a=8)
            for dc in range(8):
                S.op('pe', lambda e, p=p, dc=dc, pT=pT: e.transpose(out=pT[:, dc, :], in_=xs[p][:, dc * 128:(dc + 1) * 128], identity=ident), reads=[f'xs{p}', 'ident'], writes=[f'ps{b}'])
            evac_copy(hT[:, :, i * 128:(i + 1) * 128], pT, [f'ps{b}'], [f'hT{i}'])
        hkeys = [f'hT{i}' for i in range(4)]
        if DBG_STOP == 2:
            return

        def mmF(ft):
            b = c.nextbank()
            for dc in range(8):
                S.op('pe', lambda e, dc=dc, b=b, ft=ft: e.matmul(c.ps[b][:], lhsT=Wb[:, dc, ft * 128:(ft + 1) * 128], rhs=hT[:, dc, :], start=(dc == 0), stop=(dc == 7)), reads=hkeys + ['Wb'], writes=[f'ps{b}'])
            return b

        def store(q, dst, src, key):
            S.dma(q, lambda e: e.dma_start(out=dst, in_=src), reads=[key])

        for (fa, fb_, dst_plain, dst_rot) in [(0, 1, c.QT_d[0:128], c.QrT_d[0:128]), (2, 3, c.QT_d[128:256], c.QrT_d[128:256]),
                                              (4, 5, c.QT_d[256:384], c.QrT_d[256:384]), (6, 7, c.QT_d[384:512], c.QrT_d[384:512]),
                                              (8, 9, None, c.KsrT_d), (10, 11, None, c.KwrT_d)]:
            bA = mmF(fa)
            bB = mmF(fb_)
            if DBG_STOP == 21:
                return
            if dst_plain is not None:
                k = nst()
                evac_copy(stb[k], c.ps[bA][:], [f'ps{bA}'], [f'stb{k}'])
                if DBG_STOP == 22:
                    return
                store('pool', dst_plain[:, tsl], stb[k], f'stb{k}')
            if DBG_STOP == 23:
                return
            if DBG_STOP not in (26, 27, 28):
                ka = nst()
                evac_copy(stf[ka], c.ps[bA][:], [f'ps{bA}'], [f'stf{ka}'])
                kb = nst()
                evac_copy(stf[kb], c.ps[bB][:], [f'ps{bB}'], [f'stf{kb}'])
                S.op('dve', lambda e, ka=ka: e.tensor_tensor(out=t1, in0=stf[ka], in1=cs, op=ALU.mult), reads=[f'stf{ka}', 'cs'], writes=['t1'])
                S.op('dve', lambda e, kb=kb: e.tensor_tensor(out=t2, in0=stf[kb], in1=sn, op=ALU.mult), reads=[f'stf{kb}', 'sn'], writes=['t2'])
            if DBG_STOP == 24:
                return
            if DBG_STOP == 27:
                S.op('dve', lambda e: e.memset(t2, 1.0), writes=['t2'])
                S.op('dve', lambda e, bA=bA: e.tensor_tensor(out=t1, in0=c.ps[bA][:], in1=t2, op=ALU.mult), reads=[f'ps{bA}', f'ps{bB}', 't2'], writes=['t1'])
                return
            if DBG_STOP == 28:
                S.op('dve', lambda e: e.tensor_tensor(out=t1, in0=cs, in1=sn, op=ALU.mult), reads=['cs', 'sn'], writes=['t1'])
                return
            if DBG_STOP == 26:
                S.op('act', lambda e, bB=bB: e.copy(out=t2, in_=c.ps[bB][:]), reads=[f'ps{bB}'], writes=['t2'])
                return
            k = nst()
            S.op('pool', lambda e, k=k: e.tensor_tensor(out=stb[k], in0=t1, in1=t2, op=ALU.add), reads=['t1', 't2'], writes=[f'stb{k}'])
            if DBG_STOP == 25:
                return
            store('pool', dst_rot[:, tsl], stb[k], f'stb{k}')
        if DBG_STOP == 3:
            return
        for ft, dst in [(12, c.kcT_d), (13, c.vcT_d)]:
            b = mmF(ft)
            k = nst()
            evac_copy(stb[k], c.ps[b][:], [f'ps{b}'], [f'stb{k}'])
            store('pool', dst[:, tsl], stb[k], f'stb{k}')
        for ft in range(14, 28):
            b = mmF(ft)
            k = nst()
            evac_copy(stf[k], c.ps[b][:], [f'ps{b}'], [f'stf{k}'])
            store('pool', c.zTf_d[(ft - 14) * 128:(ft - 13) * 128, tsl], stf[k], f'stf{k}')
        if DBG_STOP == 4:
            return
        for i in range(4):
            tok = slice(tb * 512 + i * 128, tb * 512 + (i + 1) * 128)
            c0 = NF * 128
            b = c.nextbank()
            for dc in range(8):
                S.op('pe', lambda e, dc=dc, b=b, i=i: e.matmul(c.ps[b][:, 0:280], lhsT=hT[:, dc, i * 128:(i + 1) * 128], rhs=Wb[:, dc, c0:c0 + 280], start=(dc == 0), stop=(dc == 7)), reads=[f'hT{i}', 'Wb'], writes=[f'ps{b}'])
            k = nst()
            evac_copy(stb[k][:, 0:256], c.ps[b][:, 0:256], [f'ps{b}'], [f'stb{k}'])
            store('pool', c.V_d[tok, :], stb[k][:, 0:256], f'stb{k}')
            gk = i % 2
            S.op('act', lambda e, b=b, gk=gk: e.activation(out=sg[gk], in_=c.ps[b][:, 256:280], func=AF.Sigmoid), reads=[f'ps{b}'], writes=[f'sg{gk}'])
            store('pool', c.G_d[tok, :], sg[gk], f'sg{gk}')
            for m in range(4):
                cm = c0 + 280 + m * 512
                b = c.nextbank()
                for dc in range(8):
                    S.op('pe', lambda e, dc=dc, b=b, i=i, cm=cm: e.matmul(c.ps[b][:], lhsT=hT[:, dc, i * 128:(i + 1) * 128], rhs=Wb[:, dc, cm:cm + 512], start=(dc == 0), stop=(dc == 7)), reads=[f'hT{i}', 'Wb'], writes=[f'ps{b}'])
                k = nst()
                S.op('act', lambda e, b=b, k=k: e.activation(out=stf[k], in_=c.ps[b][:], func=AF.Sigmoid), reads=[f'ps{b}'], writes=[f'stf{k}'])
                store('pool', c.MG_d[tok, m * 512:(m + 1) * 512], stf[k], f'stf{k}')
    S.barrier()


def rope_tables(S_):
    inv = (1.0 / (10000.0 ** (np.arange(0, 64, 2, dtype=np.float32) / np.float32(64)))).astype(np.float32)
    ang = (np.arange(S_, dtype=np.float32)[:, None] * inv[None, :]).astype(np.float32)
    cosv, sinv = np.cos(ang).astype(np.float32), np.sin(ang).astype(np.float32)
    c64 = np.concatenate([cosv, cosv], 1).T
    s64 = np.concatenate([-sinv, sinv], 1).T
    return (np.ascontiguousarray(np.concatenate([c64, c64], 0)), np.ascontiguousarray(np.concatenate([s64, s64], 0)))


def pchunks(w, nchunk):
    return np.ascontiguousarray(w.reshape(nchunk, 128, -1).transpose(1, 0, 2))


def host_inputs(inp, b, S_):
    f = lambda a: np.ascontiguousarray(np.asarray(a, dtype=np.float32))
    m = {}
    m["x"] = f(inp["x"][b, :S_])
    w_in = f(inp["w_in"][0])
    cols = wcat_columns()
    wz = np.concatenate([w_in, np.zeros((D, 1), np.float32)], 1)
    m["wcat"] = pchunks(wz[:, cols], 8)
    m["g1"] = np.ascontiguousarray(f(inp["norm1_pre"][0]).reshape(8, 128).T)
    m["cosT"], m["sinT"] = rope_tables(S_)
    for t in "kv":
        m["w1" + t] = pchunks(f(inp["cmp_w1_" + t][0]), 16)
        pe = f(inp["cmp_pe_" + t][0])
        m["pe" + t] = np.ascontiguousarray(pe.reshape(16, 2, 64).transpose(1, 2, 0).reshape(128, 16))
        m["b1" + t] = np.ascontiguousarray(f(inp["cmp_b1_" + t][0]).reshape(2, 128).T)
    t_ = np.arange(S_)[:, None] // 64
    j_ = np.arange(S_ // 64)[None, :]
    tk = np.where(j_ > t_, -BIG, np.where((j_ == 0) | (j_ == t_), BIG, 0.0)).astype(np.float32)
    m["topk_bias"] = np.ascontiguousarray(tk)
    ch = lambda name: f(inp[name][0]).reshape(-1)
    pvs = np.stack([ch("mu_r"), ch("mu_k"), ch("mu_v"), ch("w0"), ch("a0"), ch("k_k"), ch("k_a"), ch("r_k"), ch("lnx_w"), ch("lnx_b")], -1)
    m["rwkv_pv"] = np.ascontiguousarray(pvs.reshape(4, 128, 10).transpose(1, 0, 2))
    lmu = np.zeros((128, 2), np.float32)
    lmu[0:32, 0] = ch("mu_w"); lmu[64:96, 0] = ch("mu_a"); lmu[0:96, 1] = ch("mu_g")
    m["rwkv_lmu"] = lmu
    lw2 = np.zeros((128, 512), np.float32)
    lw2[0:32] = f(inp["w_w2"][0]); lw2[64:96] = f(inp["w_a2"][0])
    m["rwkv_lw2"] = lw2
    wg2 = np.zeros((128, 512), np.float32)
    wg2[0:96] = f(inp["w_g2"][0])
    m["rwkv_wg2"] = wg2
    m["wa"] = pchunks(f(inp["w_branch_a"][0]), 4)
    m["wbr"] = pchunks(f(inp["w_branch_b"][0]), 4)
    m["wo"] = pchunks(f(inp["w_out"][0]), 8)
    m["gpost1"] = f(inp["norm1_post"][0]).reshape(1, 1024)
    m["gpost2"] = f(inp["norm2_post"][0]).reshape(1, 1024)
    m["g2"] = np.ascontiguousarray(f(inp["norm2_pre"][0]).reshape(8, 128).T)
    m["wg"] = pchunks(f(inp["w_gate"][0]), 8)
    m["wu"] = pchunks(f(inp["w_up"][0]), 8)
    m["wd"] = pchunks(f(inp["w_down"][0]), 22)
    w2k = f(inp["cmp_w2_k"][0])
    m["w2kd"] = pchunks(np.concatenate([w2k, w2k], 1), 2)
    m["w2v"] = pchunks(f(inp["cmp_w2_v"][0]), 2)
    return m


def phase2(c):
    nc, S, A, S_ = c.nc, c.S, c.A, c.S_
    N = S_ // 16 - 1
    NSLC = S_ // 64
    NNC = (N + 127) // 128
    c.N, c.NSLC, c.NNC = N, NSLC, NNC
    w1_d = {'k': c.din("w1k", [128, 16, 256]), 'v': c.din("w1v", [128, 16, 256])}
    pe_d = {'k': c.din("pek", [128, 16]), 'v': c.din("pev", [128, 16])}
    b1_d = {'k': c.din("b1k", [128, 2]), 'v': c.din("b1v", [128, 2])}
    w2k_d = c.din("w2kd", [128, 2, 128])
    w2v_d = c.din("w2v", [128, 2, 64])
    A.reset()
    c.ident = A.alloc([128], BF16)
    c.identf = A.alloc([128])
    make_ident(c, c.ident, c.identf)
    c.KC = [A.alloc([NNC * 128], BF16) for _ in range(2)]
    c.Vc1 = [A.alloc([NNC, 65 + NSLC], BF16) for _ in range(2)]
    c.EXP = A.alloc([S_], BF16)
    mark = A.off
    wst = [A.alloc([2048]) for _ in range(2)]
    w1b = {t: A.alloc([16, 256], BF16) for t in 'kv'}
    w2kb = A.alloc([2, 128], BF16)
    w2vb = A.alloc([2, 64], BF16)
    pef = A.alloc([16])
    peb = A.alloc([16], BF16)
    b1 = A.alloc([2])
    bias = A.alloc([2])
    ZZ = A.alloc([S_], BF16)
    u = A.alloc([512])
    u2 = A.alloc([512])
    gb = A.alloc([2, 512], BF16)
    ovf = A.alloc([NSLC])
    S.op('pool', lambda e: e.memset(c.EXP, 1.0), writes=['EXP'])
    S.op('pool', lambda e: e.affine_select(out=c.EXP, in_=c.EXP, pattern=[[1, S_]], compare_op=ALU.is_ge, fill=0.0, base=0, channel_multiplier=-64), reads=['EXP'], writes=['EXP'])
    S.op('pool', lambda e: e.affine_select(out=c.EXP, in_=c.EXP, pattern=[[-1, S_]], compare_op=ALU.is_ge, fill=0.0, base=63, channel_multiplier=64), reads=['EXP'], writes=['EXP'])
    for t in 'kv':
        load_weight_bf16(c, w1b[t], w1_d[t], 16, 256, None, wst, 'w1b' + t)
    load_weight_bf16(c, w2kb, w2k_d, 2, 128, None, wst, 'w2kb')
    load_weight_bf16(c, w2vb, w2v_d, 2, 64, None, wst, 'w2vb')
    for g in range(2):
        S.op('pool', lambda e, g=g: e.memset(c.KC[g], 0.0), writes=[f'KC{g}'])
        S.op('pool', lambda e, g=g: e.memset(c.Vc1[g], 0.0), writes=[f'Vc1{g}'])
        for ncx in range(NNC):
            nv = min(128, N - ncx * 128)
            S.op('pool', lambda e, g=g, ncx=ncx, nv=nv: e.memset(c.Vc1[g][0:nv, ncx, 64:65], 1.0), reads=[f'Vc1{g}'], writes=[f'Vc1{g}'])
            S.op('pool', lambda e: e.memset(ovf, 1.0), reads=['ovf'], writes=['ovf'])
            S.op('pool', lambda e, ncx=ncx: e.affine_select(out=ovf, in_=ovf, pattern=[[-4, NSLC]], compare_op=ALU.is_ge, fill=0.0, base=ncx * 128 + 1, channel_multiplier=1), reads=['ovf'], writes=['ovf'])
            S.op('pool', lambda e, ncx=ncx: e.affine_select(out=ovf, in_=ovf, pattern=[[4, NSLC]], compare_op=ALU.is_ge, fill=0.0, base=3 - ncx * 128, channel_multiplier=-1), reads=['ovf'], writes=['ovf'])
            S.op('pool', lambda e, g=g, ncx=ncx, nv=nv: e.tensor_copy(out=c.Vc1[g][0:nv, ncx, 65:65 + NSLC], in_=ovf[0:nv, :]), reads=['ovf', f'Vc1{g}'], writes=[f'Vc1{g}'])
    for t, src_d in (('k', c.kcT_d), ('v', c.vcT_d)):
        S.dma('sp', lambda e, t=t: e.dma_start(out=pef, in_=pe_d[t]), writes=['pef'])
        S.dma('sp', lambda e, t=t: e.dma_start(out=b1, in_=b1_d[t]), writes=['b1'])
        S.op('dve', lambda e: e.tensor_copy(out=peb, in_=pef), reads=['pef'], writes=['peb'])
        for hc in range(2):
            b = c.nextbank()
            for m in range(16):
                S.op('pe', lambda e, t=t, hc=hc, m=m, b=b: e.matmul(c.ps[b][:, 0:1], lhsT=w1b[t][:, m, hc * 128:(hc + 1) * 128], rhs=peb[:, m:m + 1], start=(m == 0), stop=(m == 15)), reads=['w1b' + t, 'peb'], writes=[f'ps{b}'])
            S.op('dve', lambda e, hc=hc, b=b: e.tensor_tensor(out=bias[:, hc:hc + 1], in0=c.ps[b][:, 0:1], in1=b1[:, hc:hc + 1], op=ALU.add), reads=[f'ps{b}', 'b1'], writes=[f'bias{hc}'])
        for g in range(2):
            S.dma('sp', lambda e, g=g, src_d=src_d: e.dma_start(out=ZZ[0:64, :], in_=src_d[g * 64:(g + 1) * 64, :]), writes=['ZZa'])
            S.dma('sp', lambda e, g=g, src_d=src_d: e.dma_start(out=ZZ[64:128, 0:S_ - 1], in_=src_d[g * 64:(g + 1) * 64, 1:S_]), writes=['ZZb'])
            for hc in range(2):
                b = c.nextbank()
                for m in range(16):
                    S.op('pe', lambda e, t=t, hc=hc, m=m, b=b: e.matmul(c.ps[b][:, 0:N], lhsT=w1b[t][:, m, hc * 128:(hc + 1) * 128], rhs=ZZ[:, 2 * m:2 * m + 16 * (N - 1) + 1:16], start=(m == 0), stop=(m == 15)), reads=['w1b' + t, 'ZZa', 'ZZb'], writes=[f'ps{b}'])
                S.op('act', lambda e, hc=hc, b=b: e.activation(out=u[:, 0:N], in_=c.ps[b][:, 0:N], func=AF.Identity, bias=bias[:, hc:hc + 1]), reads=[f'ps{b}', f'bias{hc}'], writes=['u'])
                S.op('pool', lambda e: e.tensor_tensor(out=u2[:, 0:N], in0=u[:, 0:N], in1=u[:, 0:N], op=ALU.mult), reads=['u'], writes=['u2'])
                S.op('dve', lambda e: e.tensor_scalar(out=u2[:, 0:N], in0=u2[:, 0:N], scalar1=0.044715, scalar2=1.0, op0=ALU.mult, op1=ALU.add), reads=['u2'], writes=['u2'])
                S.op('pool', lambda e: e.tensor_tensor(out=u2[:, 0:N], in0=u2[:, 0:N], in1=u[:, 0:N], op=ALU.mult), reads=['u', 'u2'], writes=['u2'])
                S.op('act', lambda e: e.activation(out=u2[:, 0:N], in_=u2[:, 0:N], func=AF.Tanh, scale=0.7978845608028654), reads=['u2'], writes=['u2'])
                S.op('dve', lambda e: e.scalar_tensor_tensor(out=u2[:, 0:N], in0=u2[:, 0:N], scalar=1.0, in1=u[:, 0:N], op0=ALU.add, op1=ALU.mult), reads=['u', 'u2'], writes=['u2'])
                S.op('act', lambda e, hc=hc: e.activation(out=gb[:, hc, 0:N], in_=u2[:, 0:N], func=AF.Copy, scale=0.5), reads=['u2'], writes=[f'gb{hc}'])
            if t == 'k':
                b = c.nextbank()
                for hc in range(2):
                    S.op('pe', lambda e, hc=hc, b=b: e.matmul(c.ps[b][:, 0:N], lhsT=w2kb[:, hc, :], rhs=gb[:, hc, 0:N], start=(hc == 0), stop=(hc == 1)), reads=['w2kb', 'gb0', 'gb1'], writes=[f'ps{b}'])
                S.op('act', lambda e, g=g, b=b: e.copy(out=c.KC[g][:, 0:N], in_=c.ps[b][:, 0:N]), reads=[f'ps{b}', f'KC{g}'], writes=[f'KC{g}'])
            else:
                for ncx in range(NNC):
                    nv = min(128, N - ncx * 128)
                    b = c.nextbank()
                    for hc in range(2):
                        S.op('pe', lambda e, hc=hc, b=b, ncx=ncx, nv=nv: e.matmul(c.ps[b][0:nv, 0:64], lhsT=gb[:, hc, ncx * 128:ncx * 128 + nv], rhs=w2vb[:, hc, :], start=(hc == 0), stop=(hc == 1)), reads=['w2vb', 'gb0', 'gb1'], writes=[f'ps{b}'])
                    S.op('act', lambda e, g=g, b=b, ncx=ncx, nv=nv: e.copy(out=c.Vc1[g][0:nv, ncx, 0:64], in_=c.ps[b][0:nv, 0:64]), reads=[f'ps{b}', f'Vc1{g}'], writes=[f'Vc1{g}'])
    if c.dbg:
        kcd = c.dsc("KCdbg", [2, 128, NNC * 128], BF16)
        vcd = c.dsc("VCdbg", [2, 128, NNC * (65 + NSLC)], BF16)
        for g in range(2):
            S.dma('sp', lambda e, g=g: e.dma_start(out=kcd[g], in_=c.KC[g]), reads=[f'KC{g}'])
            S.dma('sp', lambda e, g=g: e.dma_start(out=vcd[g], in_=c.Vc1[g].rearrange("p a b -> p (a b)")), reads=[f'Vc1{g}'])
    S.barrier()
    A.off = mark


def phase3(c):
    nc, S, A, S_ = c.nc, c.S, c.A, c.S_
    N, NSLC, NNC, NT, NB = c.N, c.NSLC, c.NNC, c.NT, c.NB
    c.OA_d = c.dsc("OA", [S_, 512])
    KsD = A.alloc([S_], BF16)
    KwD = A.alloc([S_], BF16)
    Vs1 = A.alloc([NT, 65], BF16)
    Vw1 = A.alloc([NT, 65], BF16)
    QP = [A.alloc([512], BF16) for _ in range(2)]
    QR = [A.alloc([512], BF16) for _ in range(2)]
    Gt = A.alloc([4, 24])
    TKt = A.alloc([4, NSLC])
    tk_d = c.din("topk_bias", [S_, NSLC])
    NE = 3
    E = [A.alloc([512], BF16) for _ in range(NE)]
    oacc = [A.alloc([256]) for _ in range(4)]
    imp = [A.alloc([NSLC]) for _ in range(4)]
    imp2 = A.alloc([NSLC])
    nmf = A.alloc([NSLC])
    m8a = A.alloc([8])
    m8b = A.alloc([8])
    rz = [A.alloc([1]) for _ in range(4)]
    cc = [A.alloc([1]) for _ in range(4)]
    NMT = A.alloc([512], BF16)
    S.op('pool', lambda e: e.memset(NMT, 0.0), writes=['NMT'])
    S.op('pool', lambda e: e.memset(Vs1, 1.0), writes=['Vs1'])
    S.op('pool', lambda e: e.memset(Vw1, 1.0), writes=['Vw1'])
    ek = [0]
    sk = [0]

    def sbank():
        sk[0] = (sk[0] + 1) % 3
        return sk[0]

    for g in range(2):
        for half in range(2):
            S.dma('sp', lambda e, g=g, half=half: e.dma_start(out=KsD[half * 64:(half + 1) * 64, :], in_=c.KsrT_d[g * 64:(g + 1) * 64, :]), writes=[f'KsD{half}'])
            S.dma('sp', lambda e, g=g, half=half: e.dma_start(out=KwD[half * 64:(half + 1) * 64, :], in_=c.KwrT_d[g * 64:(g + 1) * 64, :]), writes=[f'KwD{half}'])
        for c0 in range(0, NT, 8):
            c1 = min(NT, c0 + 8)
            S.dma('sp', lambda e, g=g, c0=c0, c1=c1: e.dma_start(out=Vs1[:, c0:c1, 0:64], in_=c.V_d[c0 * 128:c1 * 128, g * 64:(g + 1) * 64].rearrange("(c p) d -> p c d", p=128)), reads=['Vs1'], writes=['Vs1'])
            S.dma('sp', lambda e, g=g, c0=c0, c1=c1: e.dma_start(out=Vw1[:, c0:c1, 0:64], in_=c.V_d[c0 * 128:c1 * 128, 128 + g * 64:128 + (g + 1) * 64].rearrange("(c p) d -> p c d", p=128)), reads=['Vw1'], writes=['Vw1'])
        kkeys = ['KsD0', 'KsD1', 'KwD0', 'KwD1']
        for qb in range(NB):
            q0 = qb * 512
            for hp in range(2):
                rows = slice(g * 256 + hp * 128, g * 256 + (hp + 1) * 128)
                S.dma('sp', lambda e, hp=hp, rows=rows, q0=q0: e.dma_start(out=QP[hp], in_=c.QT_d[rows, q0:q0 + 512]), writes=[f'QP{hp}'])
                S.dma('sp', lambda e, hp=hp, rows=rows, q0=q0: e.dma_start(out=QR[hp], in_=c.QrT_d[rows, q0:q0 + 512]), writes=[f'QR{hp}'])
            S.dma('sp', lambda e, q0=q0: e.dma_start(out=Gt, in_=c.G_d[q0:q0 + 512, :].rearrange("(j p) n -> p j n", p=128)), writes=['Gt'])
            S.dma('sp', lambda e, q0=q0: e.dma_start(out=TKt, in_=tk_d[q0:q0 + 512, :].rearrange("(j p) n -> p j n", p=128)), writes=['TKt'])

            def branch(kind, hloc, chunks):
                hp, hl = hloc // 2, hloc % 2
                hs = slice(hl * 64, (hl + 1) * 64)
                W = 65 + NSLC if kind == 'cmp' else 65
                qt, qk = (QP[hp], f'QP{hp}') if kind == 'cmp' else (QR[hp], f'QR{hp}')
                for ci, kc in enumerate(chunks):
                    b = sbank()
                    ks = slice(kc * 128, (kc + 1) * 128)
                    if kind == 'cmp':
                        lhs, lk, Vt, vk = c.KC[g][hs, ks], f'KC{g}', c.Vc1[g][:, kc, :], f'Vc1{g}'
                    elif kind == 'slc':
                        lhs, lk, Vt, vk = KsD[hs, ks], f'KsD{hl}', Vs1[:, kc, :], 'Vs1'
                    else:
                        lhs, lk, Vt, vk = KwD[hs, ks], f'KwD{hl}', Vw1[:, kc, :], 'Vw1'
                    S.op('pe', lambda e, b=b, lhs=lhs: e.matmul(c.ps[b][:], lhsT=lhs, rhs=qt[hs, :], start=True, stop=(kind != 'slc')), reads=[lk, qk], writes=[f'ps{b}'])
                    if kind == 'slc':
                        S.op('pe', lambda e, b=b, ks=ks: e.matmul(c.ps[b][:], lhsT=c.EXP[:, ks], rhs=NMT, start=False, stop=True), reads=['EXP', 'NMT'], writes=[f'ps{b}'])
                    k = ek[0]
                    ek[0] = (k + 1) % NE
                    S.op('act', lambda e, b=b, k=k: e.activation(out=E[k], in_=c.ps[b][:], func=AF.Exp, scale=SCALE), reads=[f'ps{b}'], writes=[f'E{k}'])
                    sels = []
                    if kind == 'cmp':
                        if 16 * (kc * 128 + 127) + 31 > q0:
                            sels.append(([[1, 512]], q0 - 31 - 2048 * kc, -16, ALU.is_ge))
                    else:
                        if kc >= 4 * qb:
                            sels.append(([[1, 512]], q0 - 128 * kc, -1, ALU.is_ge))
                        if kind == 'win' and kc <= 4 * qb - 1:
                            sels.append(([[-1, 512]], 128 * kc - q0 + 512, 1, ALU.is_gt))
                    for (pat, base, cm, op) in sels:
                        S.op('pool', lambda e, k=k, pat=pat, base=base, cm=cm, op=op: e.affine_select(out=E[k], in_=E[k], pattern=pat, compare_op=op, fill=0.0, base=base, channel_multiplier=cm), reads=[f'E{k}'], writes=[f'E{k}'])
                    for jq in range(4):
                        S.op('pe', lambda e, jq=jq, k=k, Vt=Vt, W=W, ci=ci: e.matmul(c.ps[4 + jq][:, 0:W], lhsT=E[k][:, jq * 128:(jq + 1) * 128], rhs=Vt, start=(ci == 0), stop=(ci == len(chunks) - 1)), reads=[f'E{k}', vk], writes=[f'ps{4 + jq}'])
                br = {'cmp': 0, 'slc': 1, 'win': 2}[kind]
                gi = br * 8 + g * 4 + hloc
                ocols = slice(hloc * 64, (hloc + 1) * 64)
                for jq in range(4):
                    po = c.ps[4 + jq]
                    pk = f'ps{4 + jq}'
                    S.op('dve', lambda e, jq=jq, po=po: e.tensor_scalar(out=rz[jq], in0=po[:, 64:65], scalar1=1e-30, scalar2=None, op0=ALU.max), reads=[pk], writes=[f'rz{jq}'])
                    S.op('dve', lambda e, jq=jq: e.reciprocal(out=rz[jq], in_=rz[jq]), reads=[f'rz{jq}'], writes=[f'rz{jq}'])
                    S.op('dve', lambda e, jq=jq: e.tensor_tensor(out=cc[jq], in0=rz[jq], in1=Gt[:, jq, gi:gi + 1], op=ALU.mult), reads=[f'rz{jq}', 'Gt'], writes=[f'cc{jq}'])
                    if kind == 'cmp':
                        S.op('act', lambda e, jq=jq, po=po: e.activation(out=oacc[jq][:, ocols], in_=po[:, 0:64], func=AF.Copy, scale=cc[jq]), reads=[pk, f'cc{jq}'], writes=[f'oacc{jq}'])
                        if hloc == 0:
                            S.op('dve', lambda e, jq=jq, po=po: e.tensor_scalar(out=imp[jq], in0=po[:, 65:65 + NSLC], scalar1=rz[jq], scalar2=None, op0=ALU.mult), reads=[pk, f'rz{jq}'], writes=[f'imp{jq}'])
                        else:
                            S.op('dve', lambda e, jq=jq, po=po: e.scalar_tensor_tensor(out=imp[jq], in0=po[:, 65:65 + NSLC], scalar=rz[jq], in1=imp[jq], op0=ALU.mult, op1=ALU.add), reads=[pk, f'rz{jq}', f'imp{jq}'], writes=[f'imp{jq}'])
                    else:
                        S.op('dve', lambda e, jq=jq, po=po: e.scalar_tensor_tensor(out=oacc[jq][:, ocols], in0=po[:, 0:64], scalar=cc[jq], in1=oacc[jq][:, ocols], op0=ALU.mult, op1=ALU.add), reads=[pk, f'cc{jq}', f'oacc{jq}'], writes=[f'oacc{jq}'])

            cmp_chunks = [n_ for n_ in range(NNC) if 16 * (n_ * 128) + 31 <= q0 + 511]
            for hloc in range(4):
                branch('cmp', hloc, cmp_chunks)
            for jq in range(4):
                q0s = q0 + jq * 128
                S.op('dve', lambda e, jq=jq: e.tensor_tensor(out=imp[jq], in0=imp[jq], in1=TKt[:, jq, :], op=ALU.add), reads=[f'imp{jq}', 'TKt'], writes=[f'imp{jq}'])
                S.op('dve', lambda e, jq=jq: e.max(out=m8a, in_=imp[jq]), reads=[f'imp{jq}'], writes=['m8a'])
                S.op('dve', lambda e, jq=jq: e.match_replace(out=imp2, in_to_replace=m8a, in_values=imp[jq], imm_value=-BIG), reads=[f'imp{jq}', 'm8a'], writes=['imp2'])
                S.op('dve', lambda e: e.max(out=m8b, in_=imp2), reads=['imp2'], writes=['m8b'])
                S.op('dve', lambda e, jq=jq: e.tensor_scalar(out=nmf, in0=imp[jq], scalar1=m8b[:, 7:8], scalar2=None, op0=ALU.is_ge), reads=[f'imp{jq}', 'm8b'], writes=['nmf'])
                S.op('dve', lambda e: e.tensor_scalar(out=nmf, in0=nmf, scalar1=-1.0, scalar2=30000.0, op0=ALU.add, op1=ALU.mult), reads=['nmf'], writes=['nmf'])
                S.op('pe', lambda e: e.transpose(out=c.ps[3][0:NSLC, 0:128], in_=nmf, identity=c.identf), reads=['nmf', 'identf'], writes=['ps3'])
                S.op('act', lambda e, jq=jq: e.copy(out=NMT[0:NSLC, jq * 128:(jq + 1) * 128], in_=c.ps[3][0:NSLC, 0:128]), reads=['ps3', 'NMT'], writes=['NMT'])
            for hloc in range(4):
                branch('slc', hloc, list(range(0, 4 * qb + 4)))
                branch('win', hloc, list(range(max(0, 4 * qb - 4), 4 * qb + 4)))
            for jq in range(4):
                q0s = q0 + jq * 128
                S.dma('pool', lambda e, jq=jq, q0s=q0s, g=g: e.dma_start(out=c.OA_d[q0s:q0s + 128, g * 256:(g + 1) * 256], in_=oacc[jq]), reads=[f'oacc{jq}'])
    S.barrier()


def phase4(c):
    nc, S, A, S_ = c.nc, c.S, c.A, c.S_
    TB = min(1024, S_)
    NCHB = TB // 64
    c.OBT_d = c.dsc("OBT", [512, S_])
    pv_d = c.din("rwkv_pv", [128, 4, 10])
    lmu_d = c.din("rwkv_lmu", [128, 2])
    lw2_d = c.din("rwkv_lw2", [128, 512])
    wg2_d = c.din("rwkv_wg2", [128, 512])
    A.reset()
    identf = A.alloc([128])
    S.op('pool', lambda e: e.memset(identf, 1.0), writes=['identf'])
    S.op('pool', lambda e: e.affine_select(out=identf, in_=identf, pattern=[[-1, 128]], compare_op=ALU.is_equal, fill=0.0, base=0, channel_multiplier=1), reads=['identf'], writes=['identf'])
    bones = A.alloc([128])
    S.op('pool', lambda e: e.memset(bones, 0.0), writes=['bones'])
    S.op('pool', lambda e: e.memset(bones[0:64, 0:64], 1.0), reads=['bones'], writes=['bones'])
    S.op('pool', lambda e: e.memset(bones[64:128, 64:128], 1.0), reads=['bones'], writes=['bones'])
    MB = A.alloc([128]); MK = A.alloc([128]); MT = A.alloc([64])
    for (M, val) in ((MB, -1.0), (MK, 1.0)):
        S.op('pool', lambda e, M=M, val=val: e.memset(M[0:64, :], val), writes=['M' + str(val)])
        S.op('pool', lambda e, M=M: e.affine_select(out=M[0:64, 0:64], in_=M[0:64, 0:64], pattern=[[1, 64]], compare_op=ALU.is_gt, fill=0.0, base=0, channel_multiplier=-1), reads=['M' + str(val)], writes=['M' + str(val)])
        S.op('pool', lambda e, M=M: e.affine_select(out=M[0:64, 64:128], in_=M[0:64, 64:128], pattern=[[1, 64]], compare_op=ALU.is_ge, fill=0.0, base=0, channel_multiplier=-1), reads=['M' + str(val)], writes=['M' + str(val)])
    S.op('pool', lambda e: e.memset(MT[0:64, :], -1.0), writes=['MT'])
    S.op('pool', lambda e: e.affine_select(out=MT[0:64, :], in_=MT[0:64, :], pattern=[[-1, 64]], compare_op=ALU.is_gt, fill=0.0, base=0, channel_multiplier=1), reads=['MT'], writes=['MT'])
    mkeys = ['M-1.0', 'M1.0', 'MT']
    pv = A.alloc([4, 10]); lmu = A.alloc([2]); lw2 = A.alloc([512]); wg2 = A.alloc([512])
    opv = A.alloc([4, 10]); olmu = A.alloc([2])
    S.dma('sp', lambda e: e.dma_start(out=pv, in_=pv_d), writes=['pv'])
    S.dma('sp', lambda e: e.dma_start(out=lmu, in_=lmu_d), writes=['lmu'])
    S.dma('sp', lambda e: e.dma_start(out=lw2, in_=lw2_d), writes=['lw2'])
    S.dma('sp', lambda e: e.dma_start(out=wg2, in_=wg2_d), writes=['wg2'])
    S.op('dve', lambda e: e.tensor_scalar(out=opv, in0=pv, scalar1=-1.0, scalar2=1.0, op0=ALU.mult, op1=ALU.add), reads=['pv'], writes=['opv'])
    S.op('dve', lambda e: e.tensor_scalar(out=olmu, in0=lmu, scalar1=-1.0, scalar2=1.0, op0=ALU.mult, op1=ALU.add), reads=['lmu'], writes=['olmu'])
    msk = A.alloc([TB])
    S.op('pool', lambda e: e.memset(msk, 1.0), writes=['msk'])
    S.op('pool', lambda e: e.memset(msk.rearrange("p (c t) -> p c t", t=64)[:, :, 0:1], 0.0), reads=['msk'], writes=['msk'])
    H = [A.alloc([64]) for _ in range(4)]
    for hp in range(4):
        S.op('pool', lambda e, hp=hp: e.memset(H[hp], 0.0), writes=[f'H{hp}0', f'H{hp}1'])
    names = ['xr', 'xk', 'xv', 'xl', 'xg', 'tmp', 'rs', 'ks', 'vs', 'ls', 'gs', 'lw', 'av', 'gv', 'kk', 'k2', 'be', 'bon', 'L', 'eL', 'Kp', 'Bp', 'ob']
    B = {n: A.alloc([TB + 1]) for n in names}
    KBt = A.alloc([NCHB, 2, 64]); KRt = A.alloc([NCHB, 2, 64])
    G1 = A.alloc([128]); G2 = A.alloc([128]); X = [A.alloc([64]) for _ in range(2)]; XT = [A.alloc([64]) for _ in range(2)]
    R = [A.alloc([64]) for _ in range(2)]; RT = [A.alloc([64]) for _ in range(2)]
    TM = A.alloc([3, 128]); Xs = A.alloc([64]); Us = A.alloc([64]); yn = A.alloc([128])
    st6 = A.alloc([6]); mv = A.alloc([2])
    ynn = A.alloc([128])
    ynh = [A.alloc([64]) for _ in range(2)]
    S.op('pool', lambda e: e.memset(ynn, 0.0), writes=['ynz', 'ynn0', 'ynn1'])
    T_ = slice(0, TB)

    def V(eng, name, reads, writes, **kw):
        S.op(eng, lambda e: getattr(e, name)(**kw), reads=reads, writes=writes)

    def mm512(dst_fn, lhsT, rhs_buf, rkeys, rows):
        for c0 in range(0, TB, 512):
            cs_ = slice(c0, min(TB, c0 + 512))
            b = c.nextbank()
            w = cs_.stop - cs_.start
            S.op('pe', lambda e, b=b, cs_=cs_, w=w: e.matmul(c.ps[b][:, 0:w], lhsT=lhsT, rhs=rhs_buf[rows, cs_], start=True, stop=True), reads=rkeys, writes=[f'ps{b}'])
            dst_fn(c.ps[b][:, 0:w], cs_, b)

    for hp in range(4):
        P = lambda k: pv[:, hp, k:k + 1]
        OP = lambda k: opv[:, hp, k:k + 1]
        for blk in range(S_ // TB):
            t0 = blk * TB
            for n, r0 in (('xr', hp * 128), ('xk', 512 + hp * 128), ('xv', 1024 + hp * 128), ('xl', 1536), ('xg', 1664)):
                if t0 == 0:
                    V('pool', 'memset', [], [n], ap=B[n][:, 0:1], constant=0.0)
                    S.dma('sp', lambda e, n=n, r0=r0: e.dma_start(out=B[n][:, 1:TB + 1], in_=c.zTf_d[r0:r0 + 128, 0:TB]), reads=[n], writes=[n])
                else:
                    S.dma('sp', lambda e, n=n, r0=r0, t0=t0: e.dma_start(out=B[n], in_=c.zTf_d[r0:r0 + 128, t0 - 1:t0 + TB]), writes=[n])
            for src, dst, mu, omu in (('xr', 'rs', P(0), OP(0)), ('xk', 'ks', P(1), OP(1)), ('xv', 'vs', P(2), OP(2)), ('xl', 'ls', lmu[:, 0:1], olmu[:, 0:1]), ('xg', 'gs', lmu[:, 1:2], olmu[:, 1:2])):
                V('dve', 'tensor_scalar', [src, 'pv', 'lmu'], ['tmp'], out=B['tmp'][:, T_], in0=B[src][:, 0:TB], scalar1=mu, scalar2=None, op0=ALU.mult)
                V('dve', 'scalar_tensor_tensor', [src, 'tmp', 'opv', 'olmu'], [dst], out=B[dst][:, T_], in0=B[src][:, 1:TB + 1], scalar=omu, in1=B['tmp'][:, T_], op0=ALU.mult, op1=ALU.add)
            V('act', 'activation', ['ls'], ['ls', 'lsw'], out=B['ls'][0:32, T_], in_=B['ls'][0:32, T_], func=AF.Tanh)
            V('act', 'activation', ['gs'], ['gs'], out=B['gs'][0:96, T_], in_=B['gs'][0:96, T_], func=AF.Sigmoid)
            hc = slice(hp * 128, (hp + 1) * 128)
            mm512(lambda ps_, cs_, b: V('act', 'activation', [f'ps{b}', 'pv'], ['lw'], out=B['lw'][:, cs_], in_=ps_, func=AF.Sigmoid, bias=P(3)), lw2[0:32, hc], B['ls'], ['lsw', 'lw2'], slice(0, 32))
            V('dve', 'tensor_scalar', ['lw'], ['lw'], out=B['lw'][:, T_], in0=B['lw'][:, T_], scalar1=-0.6065306597126334, scalar2=None, op0=ALU.mult)
            mm512(lambda ps_, cs_, b: V('act', 'activation', [f'ps{b}', 'pv'], ['av'], out=B['av'][:, cs_], in_=ps_, func=AF.Sigmoid, bias=P(4)), lw2[64:96, hc], B['ls'], ['ls', 'lw2'], slice(64, 96))
            mm512(lambda ps_, cs_, b: V('act', 'copy', [f'ps{b}'], ['gv'], out=B['gv'][:, cs_], in_=ps_), wg2[0:96, hc], B['gs'], ['gs', 'wg2'], slice(0, 96))
            V('dve', 'tensor_scalar', ['ks', 'pv'], ['kk'], out=B['kk'][:, T_], in0=B['ks'][:, T_], scalar1=P(5), scalar2=None, op0=ALU.mult)
            V('pool', 'tensor_tensor', ['kk'], ['tmp'], out=B['tmp'][:, T_], in0=B['kk'][:, T_], in1=B['kk'][:, T_], op=ALU.mult)
            mm512(lambda ps_, cs_, b: V('act', 'activation', [f'ps{b}'], ['be'], out=B['be'][:, cs_], in_=ps_, func=AF.Sqrt), bones, B['tmp'], ['tmp', 'bones'], slice(0, 128))
            V('dve', 'tensor_scalar', ['be'], ['be'], out=B['be'][:, T_], in0=B['be'][:, T_], scalar1=1e-12, scalar2=None, op0=ALU.max)
            V('dve', 'reciprocal', ['be'], ['be'], out=B['be'][:, T_], in_=B['be'][:, T_])
            V('dve', 'tensor_tensor', ['kk', 'be'], ['kk'], out=B['kk'][:, T_], in0=B['kk'][:, T_], in1=B['be'][:, T_], op=ALU.mult)
            V('dve', 'tensor_scalar', ['av', 'pv', 'opv'], ['tmp'], out=B['tmp'][:, T_], in0=B['av'][:, T_], scalar1=P(6), scalar2=OP(6), op0=ALU.mult, op1=ALU.add)
            V('dve', 'tensor_tensor', ['ks', 'tmp'], ['k2'], out=B['k2'][:, T_], in0=B['ks'][:, T_], in1=B['tmp'][:, T_], op=ALU.mult)
            V('pool', 'tensor_tensor', ['kk', 'av'], ['be'], out=B['be'][:, T_], in0=B['kk'][:, T_], in1=B['av'][:, T_], op=ALU.mult)
            V('pool', 'tensor_tensor', ['rs', 'k2'], ['tmp'], out=B['tmp'][:, T_], in0=B['rs'][:, T_], in1=B['k2'][:, T_], op=ALU.mult)
            V('dve', 'tensor_scalar', ['tmp', 'pv'], ['tmp'], out=B['tmp'][:, T_], in0=B['tmp'][:, T_], scalar1=P(7), scalar2=None, op0=ALU.mult)
            mm512(lambda ps_, cs_, b: V('dve', 'tensor_tensor', [f'ps{b}', 'vs'], ['bon'], out=B['bon'][:, cs_], in0=ps_, in1=B['vs'][:, cs_], op=ALU.mult), bones, B['tmp'], ['tmp', 'bones'], slice(0, 128))
            V('dve', 'tensor_tensor_scan', ['lw', 'msk'], ['L'], out=B['L'][:, T_], data0=msk, data1=B['lw'][:, T_], initial=0.0, op0=ALU.mult, op1=ALU.add)
            V('act', 'activation', ['L'], ['eL'], out=B['eL'][:, T_], in_=B['L'][:, T_], func=AF.Exp)
            v3 = lambda ap: ap[:, T_].rearrange("p (c t) -> p c t", t=64)
            V('dve', 'tensor_tensor', ['rs', 'eL'], ['KRt'], out=KRt[:, :, 1, :], in0=v3(B['rs']), in1=v3(B['eL']), op=ALU.mult)
            V('pool', 'tensor_tensor', ['L', 'lw'], ['tmp'], out=B['tmp'][:, T_], in0=B['L'][:, T_], in1=B['lw'][:, T_], op=ALU.subtract)
            V('act', 'activation', ['tmp'], ['tmp'], out=B['tmp'][:, T_], in_=B['tmp'][:, T_], func=AF.Exp)
            V('dve', 'tensor_tensor', ['kk', 'tmp', 'KRt'], ['KRt'], out=KRt[:, :, 0, :], in0=v3(B['kk']), in1=v3(B['tmp']), op=ALU.mult)
            V('act', 'activation', ['L', 'tmp'], ['tmp'], out=B['tmp'][:, T_], in_=B['L'][:, T_], func=AF.Exp, scale=-1.0)
            V('dve', 'tensor_tensor', ['be', 'tmp'], ['KBt'], out=KBt[:, :, 0, :], in0=v3(B['be']), in1=v3(B['tmp']), op=ALU.mult)
            V('dve', 'tensor_tensor', ['k2', 'tmp', 'KBt'], ['KBt'], out=KBt[:, :, 1, :], in0=v3(B['k2']), in1=v3(B['tmp']), op=ALU.mult)
            V('dve', 'tensor_tensor', ['L', 'tmp'], ['tmp'], out=v3(B['tmp']), in0=v3(B['L'])[:, :, 63:64].to_broadcast([128, NCHB, 64]), in1=v3(B['L']), op=ALU.subtract)
            V('act', 'activation', ['tmp'], ['tmp'], out=B['tmp'][:, T_], in_=B['tmp'][:, T_], func=AF.Exp)
            V('dve', 'tensor_tensor', ['k2', 'tmp'], ['Kp'], out=B['Kp'][:, T_], in0=B['k2'][:, T_], in1=B['tmp'][:, T_], op=ALU.mult)
            V('dve', 'scalar_tensor_tensor', ['be', 'tmp'], ['Bp'], out=B['Bp'][:, T_], in0=B['be'][:, T_], scalar=-1.0, in1=B['tmp'][:, T_], op0=ALU.mult, op1=ALU.mult)
            if c.dbg and hp == 0 and blk == 0 and DBG_STOP == 99:
                dnames = ['rs', 'ks', 'vs', 'lw', 'av', 'gv', 'kk', 'k2', 'be', 'bon', 'L', 'Kp', 'Bp']
                c.dbgR = c.dsc("dbgR", [len(dnames), 128, TB])
                for di, n in enumerate(dnames):
                    S.dma('sp', lambda e, di=di, n=n: e.dma_start(out=c.dbgR[di], in_=B[n][:, 0:TB]), reads=[n])
                c.dbgK = c.dsc("dbgK", [2, 128, NCHB * 128])
                S.dma('sp', lambda e: e.dma_start(out=c.dbgK[0], in_=KRt.rearrange("p a b c -> p (a b c)")), reads=['KRt'])
                S.dma('sp', lambda e: e.dma_start(out=c.dbgK[1], in_=KBt.rearrange("p a b c -> p (a b c)")), reads=['KBt'])
            if DBG_STOP == 41:
                return
            for ch in range(NCHB):
                cs = slice(ch * 64, (ch + 1) * 64)
                hs = slice(0, 64)
                b = c.nextbank()
                for k, n in enumerate(('vs', 'Kp', 'Bp')):
                    S.op('pe', lambda e, hs=hs, ch=ch, hp=hp, cs=cs, b=b, k=k, n=n: e.transpose(out=c.ps[b][0:64, k * 128:(k + 1) * 128], in_=B[n][:, cs], identity=identf), reads=[n, 'identf'], writes=[f'ps{b}'])
                V('act', 'copy', [f'ps{b}'], ['TM'], out=TM[0:64].rearrange("p a b -> p (a b)"), in_=c.ps[b][0:64, 0:384])
                for hl in range(2):
                    hs = slice(hl * 64, (hl + 1) * 64)
                    Hk = f'H{hp}{hl}'
                    b1 = c.nextbank()
                    S.op('pe', lambda e, hs=hs, ch=ch, hp=hp, cs=cs, b1=b1: e.matmul(c.ps[b1][0:64, 0:128], lhsT=KBt[hs, ch, 0, :], rhs=KRt[hs, ch].rearrange("p a b -> p (a b)"), start=True, stop=True), reads=['KBt', 'KRt'], writes=[f'ps{b1}'])
                    V('dve', 'tensor_tensor', [f'ps{b1}'] + mkeys, ['G1'], out=G1[0:64], in0=c.ps[b1][0:64, 0:128], in1=MB[0:64], op=ALU.mult)
                    b2 = c.nextbank()
                    S.op('pe', lambda e, hs=hs, ch=ch, hp=hp, cs=cs, b2=b2: e.matmul(c.ps[b2][0:64, 0:128], lhsT=KBt[hs, ch, 1, :], rhs=KRt[hs, ch].rearrange("p a b -> p (a b)"), start=True, stop=True), reads=['KBt', 'KRt'], writes=[f'ps{b2}'])
                    V('dve', 'tensor_tensor', [f'ps{b2}'] + mkeys, ['G2'], out=G2[0:64], in0=c.ps[b2][0:64, 0:128], in1=MK[0:64], op=ALU.mult)
                    b3 = c.nextbank()
                    S.op('pe', lambda e, hs=hs, ch=ch, hp=hp, cs=cs, b3=b3: e.matmul(c.ps[b3][0:64, 0:64], lhsT=KRt[hs, ch, 0, :], rhs=KBt[hs, ch, 0, :], start=True, stop=True), reads=['KBt', 'KRt'], writes=[f'ps{b3}'])
                    V('dve', 'tensor_tensor', [f'ps{b3}'] + mkeys, ['XT0'], out=XT[0][0:64], in0=c.ps[b3][0:64, 0:64], in1=MT[0:64], op=ALU.mult)
                    V('pool', 'tensor_copy', ['G1'], ['X0'], out=X[0][0:64], in_=G1[0:64, 0:64])
                    V('pool', 'tensor_tensor', ['G1', 'identf'], ['R0'], out=R[0][0:64], in0=G1[0:64, 0:64], in1=identf[0:64, 0:64], op=ALU.add)
                    V('pool', 'tensor_tensor', ['XT0', 'identf'], ['RT0'], out=RT[0][0:64], in0=XT[0][0:64], in1=identf[0:64, 0:64], op=ALU.add)
                    cur = 0
                    for lev in range(5):
                        nx = 1 - cur
                        last = lev == 4
                        bx = c.nextbank()
                        S.op('pe', lambda e, hs=hs, ch=ch, hp=hp, cs=cs, bx=bx, cur=cur: e.matmul(c.ps[bx][0:64, 0:64], lhsT=XT[cur][0:64], rhs=X[cur][0:64], start=True, stop=True), reads=[f'X{cur}', f'XT{cur}'], writes=[f'ps{bx}'])
                        V('act', 'copy', [f'ps{bx}'], [f'X{nx}'], out=X[nx][0:64], in_=c.ps[bx][0:64, 0:64])
                        if not last:
                            bt = c.nextbank()
                            S.op('pe', lambda e, hs=hs, ch=ch, hp=hp, cs=cs, bt=bt, cur=cur: e.matmul(c.ps[bt][0:64, 0:64], lhsT=X[cur][0:64], rhs=XT[cur][0:64], start=True, stop=True), reads=[f'X{cur}', f'XT{cur}'], writes=[f'ps{bt}'])
                            V('act', 'copy', [f'ps{bt}'], [f'XT{nx}'], out=XT[nx][0:64], in_=c.ps[bt][0:64, 0:64])
                        br_ = c.nextbank()
                        S.op('pe', lambda e, hs=hs, ch=ch, hp=hp, cs=cs, br_=br_, cur=cur, nx=nx: e.matmul(c.ps[br_][0:64, 0:64], lhsT=RT[cur][0:64], rhs=X[nx][0:64], start=True, stop=True), reads=[f'RT{cur}', f'X{nx}'], writes=[f'ps{br_}'])
                        V('dve', 'tensor_tensor', [f'ps{br_}', f'R{cur}'], [f'R{nx}'], out=R[nx][0:64], in0=c.ps[br_][0:64, 0:64], in1=R[cur][0:64], op=ALU.add)
                        if not last:
                            bq = c.nextbank()
                            S.op('pe', lambda e, hs=hs, ch=ch, hp=hp, cs=cs, bq=bq, cur=cur, nx=nx: e.matmul(c.ps[bq][0:64, 0:64], lhsT=X[nx][0:64], rhs=RT[cur][0:64], start=True, stop=True), reads=[f'RT{cur}', f'X{nx}'], writes=[f'ps{bq}'])
                            V('dve', 'tensor_tensor', [f'ps{bq}', f'RT{cur}'], [f'RT{nx}'], out=RT[nx][0:64], in0=c.ps[bq][0:64, 0:64], in1=RT[cur][0:64], op=ALU.add)
                        cur = nx
                    if DBG_STOP == 42:
                        return
                    Rf, Rk = R[cur], f'R{cur}'
                    Vt, Kpt, Bpt = TM[0:64, 0, hs], TM[0:64, 1, hs], TM[0:64, 2, hs]
                    bX = c.nextbank()
                    S.op('pe', lambda e, hs=hs, ch=ch, hp=hp, cs=cs, Vt=Vt, Kpt=Kpt, Bpt=Bpt, bX=bX: e.matmul(c.ps[bX][0:64, 0:64], lhsT=KRt[hs, ch, 0, :], rhs=H[hp][hs, :], start=True, stop=True), reads=['KRt', Hk], writes=[f'ps{bX}'])
                    bX2 = c.nextbank()
                    S.op('pe', lambda e, hs=hs, ch=ch, hp=hp, cs=cs, Vt=Vt, Kpt=Kpt, Bpt=Bpt, bX2=bX2: e.matmul(c.ps[bX2][0:64, 0:64], lhsT=G2[0:64, 0:64], rhs=Vt, start=True, stop=True), reads=['G2', 'TM'], writes=[f'ps{bX2}'])
                    V('act', 'copy', [f'ps{bX2}'], ['Xs'], out=Xs[0:64], in_=c.ps[bX2][0:64, 0:64])
                    V('dve', 'tensor_tensor', [f'ps{bX}', 'Xs'], ['Xs'], out=Xs[0:64], in0=c.ps[bX][0:64, 0:64], in1=Xs[0:64], op=ALU.add)
                    bU = c.nextbank()
                    S.op('pe', lambda e, hs=hs, ch=ch, hp=hp, cs=cs, Vt=Vt, Kpt=Kpt, Bpt=Bpt, bU=bU, Rf=Rf: e.matmul(c.ps[bU][0:64, 0:64], lhsT=Rf[0:64], rhs=Xs[0:64], start=True, stop=True), reads=[Rk, 'Xs'], writes=[f'ps{bU}'])
                    V('act', 'copy', [f'ps{bU}'], ['Us'], out=Us[0:64], in_=c.ps[bU][0:64, 0:64])
                    bY = c.nextbank()
                    S.op('pe', lambda e, hs=hs, ch=ch, hp=hp, cs=cs, Vt=Vt, Kpt=Kpt, Bpt=Bpt, bY=bY: e.matmul(c.ps[bY][0:64, 0:64], lhsT=KRt[hs, ch, 1, :], rhs=H[hp][hs, :], start=True, stop=True), reads=['KRt', Hk], writes=[f'ps{bY}'])
                    bY2 = c.nextbank()
                    S.op('pe', lambda e, hs=hs, ch=ch, hp=hp, cs=cs, Vt=Vt, Kpt=Kpt, Bpt=Bpt, bY2=bY2: e.matmul(c.ps[bY2][0:64, 0:64], lhsT=G2[0:64, 64:128], rhs=Vt, start=True, stop=False), reads=['G2', 'TM'], writes=[f'ps{bY2}'])
                    S.op('pe', lambda e, hs=hs, ch=ch, hp=hp, cs=cs, Vt=Vt, Kpt=Kpt, Bpt=Bpt, bY2=bY2: e.matmul(c.ps[bY2][0:64, 0:64], lhsT=G1[0:64, 64:128], rhs=Us[0:64], start=False, stop=True), reads=['G1', 'Us'], writes=[f'ps{bY2}'])
                    bH = c.nextbank()
                    S.op('pe', lambda e, hs=hs, ch=ch, hp=hp, cs=cs, Vt=Vt, Kpt=Kpt, Bpt=Bpt, bH=bH: e.matmul(c.ps[bH][0:64, 0:64], lhsT=Bpt, rhs=Us[0:64], start=True, stop=False), reads=['TM', 'Us'], writes=[f'ps{bH}'])
                    S.op('pe', lambda e, hs=hs, ch=ch, hp=hp, cs=cs, Vt=Vt, Kpt=Kpt, Bpt=Bpt, bH=bH: e.matmul(c.ps[bH][0:64, 0:64], lhsT=Kpt, rhs=Vt, start=False, stop=True), reads=['TM'], writes=[f'ps{bH}'])
                    V('dve', 'scalar_tensor_tensor', [f'ps{bH}', Hk, 'eL'], [Hk], out=H[hp][hs, :], in0=H[hp][hs, :], scalar=B['eL'][hs, ch * 64 + 63:ch * 64 + 64], in1=c.ps[bH][0:64, 0:64], op0=ALU.mult, op1=ALU.add)
                    if DBG_STOP == 98:
                        c.dbgC = c.dsc("dbgC", [6, 64, 128])
                        for di, (ap_, key_) in enumerate(((G1[0:64], 'G1'), (G2[0:64], 'G2'), (XT[0][0:64], 'XT0'), (Rf[0:64], Rk), (Xs[0:64], 'Xs'), (Us[0:64], 'Us'))):
                            w_ = 128 if di < 2 else 64
                            S.dma('sp', lambda e, hs=hs, ch=ch, hp=hp, cs=cs, Vt=Vt, Kpt=Kpt, Bpt=Bpt, di=di, ap_=ap_, w_=w_: e.dma_start(out=c.dbgC[di, :, 0:w_], in_=ap_), reads=[key_])
                        c.dbgT = c.dsc("dbgT", [64, 384])
                        S.dma('sp', lambda e, hs=hs, ch=ch, hp=hp, cs=cs, Vt=Vt, Kpt=Kpt, Bpt=Bpt: e.dma_start(out=c.dbgT, in_=TM[0:64].rearrange("p a b -> p (a b)")), reads=['TM'])
                        return
                    if DBG_STOP == 43:
                        return
                    V('act', 'copy', [f'ps{bY2}'], [f'yn{hl}'], out=ynh[hl][0:64], in_=c.ps[bY2][0:64, 0:64])
                    V('dve', 'tensor_tensor', [f'ps{bY}', f'yn{hl}'], [f'yn{hl}'], out=ynh[hl][0:64], in0=c.ps[bY][0:64, 0:64], in1=ynh[hl][0:64], op=ALU.add)
                    V('dve', 'bn_stats', [f'yn{hl}'], ['st6'], out=st6[0:64], in_=ynh[hl][0:64])
                    V('dve', 'bn_aggr', ['st6'], ['mv'], out=mv[0:64], in_=st6[0:64])
                    V('dve', 'tensor_scalar', ['mv'], ['mvr'], out=mv[0:64, 1:2], in0=mv[0:64, 1:2], scalar1=64e-5, scalar2=None, op0=ALU.add)
                    V('act', 'activation', ['mvr'], ['mvr'], out=mv[0:64, 1:2], in_=mv[0:64, 1:2], func=AF.Sqrt)
                    V('dve', 'reciprocal', ['mvr'], ['mvr'], out=mv[0:64, 1:2], in_=mv[0:64, 1:2])
                    V('dve', 'tensor_scalar', [f'yn{hl}', 'mv', 'mvr'], [f'ynn{hl}'], out=ynn[0:64, hs], in0=ynh[hl][0:64], scalar1=mv[0:64, 0:1], scalar2=mv[0:64, 1:2], op0=ALU.subtract, op1=ALU.mult)
                    if DBG_STOP == 96:
                        c.dbgY = c.dsc("dbgY", [64, 128])
                        S.dma('sp', lambda e, hs=hs, ch=ch, hp=hp, cs=cs, Vt=Vt, Kpt=Kpt, Bpt=Bpt: e.dma_start(out=c.dbgY, in_=yn[0:64, :]), reads=['yn0', 'yn1'])
                        c.dbgO = c.dsc("dbgO", [128, 64])
                        return
                bT = c.nextbank()
                S.op('pe', lambda e, hs=hs, ch=ch, hp=hp, cs=cs, Vt=Vt, Kpt=Kpt, Bpt=Bpt, bT=bT: e.transpose(out=c.ps[bT][:, 0:128], in_=ynn, identity=identf), reads=['ynn0', 'ynn1', 'ynz', 'identf'], writes=[f'ps{bT}'])
                V('act', 'activation', [f'ps{bT}', 'pv'], ['ob0', 'obf'], out=B['ob'][:, cs], in_=c.ps[bT][:, 0:64], func=AF.Identity, scale=pv[:, hp, 8:9], bias=pv[:, hp, 9:10])
                if DBG_STOP == 97:
                    c.dbgY = c.dsc("dbgY", [64, 128])
                    S.dma('sp', lambda e, hs=hs, ch=ch, hp=hp, cs=cs, Vt=Vt, Kpt=Kpt, Bpt=Bpt: e.dma_start(out=c.dbgY, in_=ynn[0:64, :]), reads=['ynn0', 'ynn1'])
                    c.dbgO = c.dsc("dbgO", [128, 64])
                    S.dma('sp', lambda e, hs=hs, ch=ch, hp=hp, cs=cs, Vt=Vt, Kpt=Kpt, Bpt=Bpt: e.dma_start(out=c.dbgO, in_=B['ob'][:, 0:64]), reads=['ob0'])
                    return
            V('pool', 'tensor_tensor', ['ob0', 'bon'], ['obf'], out=B['ob'][:, T_], in0=B['ob'][:, T_], in1=B['bon'][:, T_], op=ALU.add)
            V('pool', 'tensor_tensor', ['obf', 'gv'], ['obf'], out=B['ob'][:, T_], in0=B['ob'][:, T_], in1=B['gv'][:, T_], op=ALU.mult)
            S.dma('pool', lambda e, hp=hp, t0=t0: e.dma_start(out=c.OBT_d[hp * 128:(hp + 1) * 128, t0:t0 + TB], in_=B['ob'][:, T_]), reads=['obf'])
    S.barrier()


def _rstd_from_two(c, srcs, rkeys, junk, ssa, ssb, tag, dim=1024, eps=1e-6):
    S = c.S
    S.op('act', lambda e: e.activation(out=junk, in_=srcs[0], func=AF.Square, accum_out=ssa), reads=[rkeys[0]], writes=[tag + 'a', 'junkx'])
    S.op('act', lambda e: e.activation(out=junk, in_=srcs[1], func=AF.Square, accum_out=ssb), reads=[rkeys[1]], writes=[tag + 'b', 'junkx'])
    S.op('dve', lambda e: e.tensor_tensor(out=ssa, in0=ssa, in1=ssb, op=ALU.add), reads=[tag + 'a', tag + 'b'], writes=[tag + 'a'])
    S.op('dve', lambda e: e.tensor_scalar(out=ssa, in0=ssa, scalar1=1.0 / dim, scalar2=eps, op0=ALU.mult, op1=ALU.add), reads=[tag + 'a'], writes=[tag + 'a'])
    S.op('act', lambda e: e.activation(out=ssa, in_=ssa, func=AF.Sqrt), reads=[tag + 'a'], writes=[tag + 'a'])
    S.op('dve', lambda e: e.reciprocal(out=ssa, in_=ssa), reads=[tag + 'a'], writes=[tag + 'a'])


def _transposes(c, src_bf16, nblk, ident, dst, rkey, wkey):
    S = c.S
    for k0 in range(0, nblk, 8):
        k1 = min(nblk, k0 + 8)
        b = c.nextbank()
        pT = c.ps[b][:].bitcast(BF16).rearrange("p (a b) -> p a b", a=8)
        for k in range(k0, k1):
            S.op('pe', lambda e, k=k, pT=pT, k0=k0: e.transpose(out=pT[:, k - k0, :], in_=src_bf16[:, k * 128:(k + 1) * 128], identity=ident), reads=[rkey, 'ident'], writes=[f'ps{b}'])
        S.op('dve', lambda e, pT=pT, k0=k0, k1=k1: e.tensor_copy(out=dst[:, k0:k1, :], in_=pT[:, 0:k1 - k0, :]), reads=[f'ps{b}'], writes=[wkey])


def phase5(c):
    nc, S, A, S_ = c.nc, c.S, c.A, c.S_
    obt_d = getattr(c, 'OBT_d', None)
    if obt_d is None:
        obt_d = c.din("OBTin", [512, S_])
    wa_d = c.din("wa", [128, 4, 1024])
    wbr_d = c.din("wbr", [128, 4, 1024])
    wo_d = c.din("wo", [128, 8, 1024])
    gp1_d = c.din("gpost1", [1, 1024])
    gp2_d = c.din("gpost2", [1, 1024])
    g2_d = c.din("g2", [128, 8])
    wg_d = c.din("wg", [128, 8, DFF])
    wu_d = c.din("wu", [128, 8, DFF])
    wd_d = c.din("wd", [128, 22, 1024])
    X1_d = c.dsc("X1", [S_, 1024])
    A.reset()
    ident = A.alloc([128], BF16)
    identf = A.alloc([128])
    make_ident(c, ident, identf)
    wst = [A.alloc([2048]) for _ in range(2)]
    Wab = A.alloc([4, 1024], BF16)
    Wbb = A.alloc([4, 1024], BF16)
    Wob = A.alloc([8, 1024], BF16)
    gpb = A.alloc([1024])
    load_weight_bf16(c, Wab, wa_d, 4, 1024, None, wst, 'Wab')
    load_weight_bf16(c, Wbb, wbr_d, 4, 1024, None, wst, 'Wbb')
    load_weight_bf16(c, Wob, wo_d, 8, 1024, None, wst, 'Wob')
    S.dma('sp', lambda e: e.dma_start(out=gpb, in_=gp1_d.to_broadcast([128, 1024])), writes=['gpb'])
    xt = [A.alloc([1024]) for _ in range(2)]
    oabf = [A.alloc([1024]) for _ in range(2)]
    mg = [A.alloc([2048]) for _ in range(2)]
    ob16 = A.alloc([1024], BF16)
    oT = A.alloc([8, 128], BF16)
    t1 = A.alloc([512])
    t2 = A.alloc([512])
    mxb = A.alloc([1024], BF16)
    mT = A.alloc([8, 128], BF16)
    junk = A.alloc([512], BF16)
    ssa = A.alloc([1])
    ssb = A.alloc([1])
    x1 = [A.alloc([1024]) for _ in range(2)]
    for tt in range(c.NT):
        p = tt % 2
        tok = slice(tt * 128, (tt + 1) * 128)
        S.dma('sp', lambda e, p=p, tok=tok: e.dma_start(out=oabf[p][:, 0:512], in_=c.OA_d[tok, :]), writes=[f'oaf{p}'])
        S.dma('sp', lambda e, p=p, tok=tok: e.dma_start(out=oabf[p][:, 512:1024].rearrange("p (a b) -> p a b", a=4), in_=obt_d[:, tok].rearrange("(a p) t -> p a t", p=128)), writes=[f'obf{p}'])
        S.dma('sp', lambda e, p=p, tok=tok: e.dma_start(out=mg[p], in_=c.MG_d[tok, :]), writes=[f'mg{p}'])
        S.dma('sp', lambda e, p=p, tok=tok: e.dma_start(out=xt[p], in_=c.x_d[tok, :]), writes=[f'xt{p}'])
        S.op('act', lambda e, p=p: e.copy(out=ob16[:, 0:512], in_=oabf[p][:, 0:512]), reads=[f'oaf{p}'], writes=['ob16'])
        _transposes(c, ob16, 4, ident, oT, 'ob16', 'oTa')
        S.op('act', lambda e, p=p: e.copy(out=oT[:, 4:8, :], in_=oabf[p][:, 512:1024].rearrange("p (a b) -> p a b", a=4)), reads=[f'obf{p}'], writes=['oTb'])
        for h in range(2):
            hc = slice(h * 512, (h + 1) * 512)
            bA = c.nextbank()
            for ch in range(4):
                S.op('pe', lambda e, ch=ch, bA=bA, hc=hc: e.matmul(c.ps[bA][:], lhsT=oT[:, ch, :], rhs=Wab[:, ch, hc], start=(ch == 0), stop=(ch == 3)), reads=['oTa', 'Wab'], writes=[f'ps{bA}'])
            bB = c.nextbank()
            for ch in range(4):
                S.op('pe', lambda e, ch=ch, bB=bB, hc=hc: e.matmul(c.ps[bB][:], lhsT=oT[:, 4 + ch, :], rhs=Wbb[:, ch, hc], start=(ch == 0), stop=(ch == 3)), reads=['oTb', 'Wbb'], writes=[f'ps{bB}'])
            S.op('dve', lambda e, bA=bA, p=p, h=h: e.tensor_tensor(out=t1, in0=c.ps[bA][:], in1=mg[p][:, h * 512:(h + 1) * 512], op=ALU.mult), reads=[f'ps{bA}', f'mg{p}'], writes=['t1'])
            S.op('dve', lambda e, bB=bB, p=p, h=h: e.tensor_tensor(out=t2, in0=c.ps[bB][:], in1=mg[p][:, 1024 + h * 512:1024 + (h + 1) * 512], op=ALU.mult), reads=[f'ps{bB}', f'mg{p}'], writes=['t2'])
            S.op('pool', lambda e, hc=hc: e.tensor_tensor(out=mxb[:, hc], in0=t1, in1=t2, op=ALU.add), reads=['t1', 't2'], writes=[f'mxb{h}'])
        _transposes(c, mxb, 8, ident, mT, 'mxb0', 'mT') if False else None
        for k0 in [0]:
            b = c.nextbank()
            pT = c.ps[b][:].bitcast(BF16).rearrange("p (a b) -> p a b", a=8)
            for k in range(8):
                S.op('pe', lambda e, k=k, pT=pT: e.transpose(out=pT[:, k, :], in_=mxb[:, k * 128:(k + 1) * 128], identity=ident), reads=['mxb0', 'mxb1', 'ident'], writes=[f'ps{b}'])
            S.op('dve', lambda e, pT=pT: e.tensor_copy(out=mT, in_=pT), reads=[f'ps{b}'], writes=['mT'])
        bh = []
        for h in range(2):
            hc = slice(h * 512, (h + 1) * 512)
            b = c.nextbank()
            bh.append(b)
            for ch in range(8):
                S.op('pe', lambda e, ch=ch, b=b, hc=hc: e.matmul(c.ps[b][:], lhsT=mT[:, ch, :], rhs=Wob[:, ch, hc], start=(ch == 0), stop=(ch == 7)), reads=['mT', 'Wob'], writes=[f'ps{b}'])
        _rstd_from_two(c, [c.ps[bh[0]][:], c.ps[bh[1]][:]], [f'ps{bh[0]}', f'ps{bh[1]}'], junk, ssa, ssb, 'ssx')
        for h in range(2):
            hc = slice(h * 512, (h + 1) * 512)
            S.op('dve', lambda e, h=h, hc=hc, bh=bh: e.scalar_tensor_tensor(out=t1, in0=c.ps[bh[h]][:], scalar=ssa, in1=gpb[:, hc], op0=ALU.mult, op1=ALU.mult), reads=[f'ps{bh[h]}', 'ssxa', 'gpb'], writes=['t1'])
            S.op('pool', lambda e, p=p, hc=hc: e.tensor_tensor(out=x1[p][:, hc], in0=t1, in1=xt[p][:, hc], op=ALU.add), reads=['t1', f'xt{p}'], writes=[f'x1{p}{h}'])
        S.dma('pool', lambda e, p=p, tok=tok: e.dma_start(out=X1_d[tok, :], in_=x1[p]), reads=[f'x1{p}0', f'x1{p}1'])
    S.barrier()
    A.reset()
    ident = A.alloc([128], BF16)
    identf = A.alloc([128])
    make_ident(c, ident, identf)
    wst = [A.alloc([2048]) for _ in range(2)]
    g2 = A.alloc([8])
    Wg = A.alloc([8, DFF], BF16)
    Wu = A.alloc([8, DFF], BF16)
    Wd = A.alloc([22, 1024], BF16)
    g2pb = A.alloc([1024])
    S.dma('sp', lambda e: e.dma_start(out=g2, in_=g2_d), writes=['gcolWg', 'gcolWu'])
    load_weight_bf16(c, Wg, wg_d, 8, DFF, g2, wst, 'Wg')
    load_weight_bf16(c, Wu, wu_d, 8, DFF, g2, wst, 'Wu')
    load_weight_bf16(c, Wd, wd_d, 22, 1024, None, wst, 'Wd')
    S.dma('sp', lambda e: e.dma_start(out=g2pb, in_=gp2_d.to_broadcast([128, 1024])), writes=['g2pb'])
    x1 = [A.alloc([1024]) for _ in range(2)]
    x1s = A.alloc([1024], BF16)
    junk = A.alloc([1024], BF16)
    ss = A.alloc([1])
    ssb = A.alloc([1])
    ssy = A.alloc([1])
    h2T = A.alloc([8, 128], BF16)
    tmp = A.alloc([512])
    actb = A.alloc([DFF], BF16)
    aT = A.alloc([22, 128], BF16)
    t1 = A.alloc([512])
    yo = [A.alloc([1024]) for _ in range(2)]
    groups = [(i * 512, 512) for i in range(5)] + [(2560, 256)]
    for tt in range(c.NT):
        p = tt % 2
        tok = slice(tt * 128, (tt + 1) * 128)
        S.dma('sp', lambda e, p=p, tok=tok: e.dma_start(out=x1[p], in_=X1_d[tok, :]), writes=[f'x1{p}'])
        rms_rstd(c, x1[p], junk, ss, D, [f'x1{p}'], 'ss5')
        S.op('act', lambda e, p=p: e.activation(out=x1s, in_=x1[p], func=AF.Copy, scale=ss), reads=[f'x1{p}', 'ss5'], writes=['x1s'])
        _transposes(c, x1s, 8, ident, h2T, 'x1s', 'h2T')
        for gi, (c0, w) in enumerate(groups):
            bg = c.nextbank()
            for ch in range(8):
                S.op('pe', lambda e, ch=ch, bg=bg, c0=c0, w=w: e.matmul(c.ps[bg][:, 0:w], lhsT=h2T[:, ch, :], rhs=Wg[:, ch, c0:c0 + w], start=(ch == 0), stop=(ch == 7)), reads=['h2T', 'Wg'], writes=[f'ps{bg}'])
            bu = c.nextbank()
            for ch in range(8):
                S.op('pe', lambda e, ch=ch, bu=bu, c0=c0, w=w: e.matmul(c.ps[bu][:, 0:w], lhsT=h2T[:, ch, :], rhs=Wu[:, ch, c0:c0 + w], start=(ch == 0), stop=(ch == 7)), reads=['h2T', 'Wu'], writes=[f'ps{bu}'])
            S.op('act', lambda e, bg=bg, w=w: e.activation(out=tmp[:, 0:w], in_=c.ps[bg][:, 0:w], func=AF.Silu), reads=[f'ps{bg}'], writes=['tmp'])
            S.op('dve', lambda e, bu=bu, c0=c0, w=w: e.tensor_tensor(out=actb[:, c0:c0 + w], in0=tmp[:, 0:w], in1=c.ps[bu][:, 0:w], op=ALU.mult), reads=['tmp', f'ps{bu}'], writes=[f'actb{gi}'])
        akeys = [f'actb{gi}' for gi in range(6)]
        for k0 in range(0, 22, 8):
            k1 = min(22, k0 + 8)
            b = c.nextbank()
            pT = c.ps[b][:].bitcast(BF16).rearrange("p (a b) -> p a b", a=8)
            for k in range(k0, k1):
                S.op('pe', lambda e, k=k, pT=pT, k0=k0: e.transpose(out=pT[:, k - k0, :], in_=actb[:, k * 128:(k + 1) * 128], identity=ident), reads=akeys + ['ident'], writes=[f'ps{b}'])
            S.op('dve', lambda e, pT=pT, k0=k0, k1=k1: e.tensor_copy(out=aT[:, k0:k1, :], in_=pT[:, 0:k1 - k0, :]), reads=[f'ps{b}'], writes=[f'aT{k0}'])
        bh = []
        for h in range(2):
            hc = slice(h * 512, (h + 1) * 512)
            b = c.nextbank()
            bh.append(b)
            for ch in range(22):
                S.op('pe', lambda e, ch=ch, b=b, hc=hc: e.matmul(c.ps[b][:], lhsT=aT[:, ch, :], rhs=Wd[:, ch, hc], start=(ch == 0), stop=(ch == 21)), reads=['aT0', 'aT8', 'aT16', 'Wd'], writes=[f'ps{b}'])
        _rstd_from_two(c, [c.ps[bh[0]][:], c.ps[bh[1]][:]], [f'ps{bh[0]}', f'ps{bh[1]}'], junk[:, 0:512], ssy, ssb, 'ssy')
        for h in range(2):
            hc = slice(h * 512, (h + 1) * 512)
            S.op('dve', lambda e, h=h, hc=hc, bh=bh: e.scalar_tensor_tensor(out=t1, in0=c.ps[bh[h]][:], scalar=ssy, in1=g2pb[:, hc], op0=ALU.mult, op1=ALU.mult), reads=[f'ps{bh[h]}', 'ssya', 'g2pb'], writes=['t1'])
            S.op('pool', lambda e, p=p, hc=hc: e.tensor_tensor(out=yo[p][:, hc], in0=t1, in1=x1[p][:, hc], op=ALU.add), reads=['t1', f'x1{p}'], writes=[f'yo{p}{h}'])
        S.dma('pool', lambda e, p=p, tok=tok: e.dma_start(out=c.y_d[tok, :], in_=yo[p]), reads=[f'yo{p}0', f'yo{p}1'])
    S.barrier()


def kernel(**inputs):
    S_ = 8192
    nc, c = build(S_, dbg=False, phases=(1, 2, 3, 4, 5))
    nb = int(np.asarray(inputs["x"]).shape[0])
    in_maps = [host_inputs(inputs, b, S_) for b in range(nb)]
    res = run_bass_kernel_spmd(nc, in_maps, core_ids=list(range(nb)))
    return np.stack([np.asarray(r["y"], dtype=np.float32) for r in res.results], 0)
```
